# Optimizing a Trainium2 kernel written in Bass

```python
import jax, jax.numpy as jnp
from jax import lax
import numpy as np

D_MODEL = 1024
BATCH = 16
SEQ = 4096
DEPTH = 1
DEC_BATCH = 128
DEC_SEQ = 1
PAST_LEN = 8192
PAGE_SIZE = 128

HEAD_DIM = 64
HEADS_PER_GROUP = 4
DIL_GROUPS = ((128, 1), (512, 4), (2048, 16))
N_HEADS_A = HEADS_PER_GROUP * len(DIL_GROUPS)
D_ATTN = N_HEADS_A * HEAD_DIM
D_GMLP = 512
GMLP_GROUPS = 4
GMLP_GROUP_DIM = D_GMLP // GMLP_GROUPS
CHUNK = 128
D_FF = 2816
N_ADA = 9
EPS = 1e-6
IN_SPLITS = (D_ATTN, 2 * D_ATTN, 3 * D_ATTN, 3 * D_ATTN + D_GMLP, 3 * D_ATTN + 2 * D_GMLP,
             3 * D_ATTN + 2 * D_GMLP + D_MODEL)
D_IN = 3 * D_ATTN + 2 * D_GMLP + 2 * D_MODEL

kernel_name = "hybrid_dilated_attn_gmlp_macaron_decode"


def rms_norm(x, g):
    xf = x.astype(jnp.float32)
    r = lax.rsqrt(jnp.mean(xf * xf, axis=-1, keepdims=True) + EPS)
    return (xf * r).astype(x.dtype) * g


def layer_norm(x, g, b):
    xf = x.astype(jnp.float32)
    mu = jnp.mean(xf, axis=-1, keepdims=True)
    var = jnp.mean(jnp.square(xf - mu), axis=-1, keepdims=True)
    return ((xf - mu) * lax.rsqrt(var + EPS)).astype(x.dtype) * g + b


def swiglu(h, w_up, w_down):
    a, b = jnp.split(h @ w_up, 2, axis=-1)
    return (jax.nn.silu(a) * b) @ w_down


def dilated_attn_prompt(q, k, v, window, dil):
    B, S, H, dh = q.shape
    nk = window // dil
    span = nk * dil
    s_pad = -(-S // span) * span
    nb = s_pad // span

    def to_blocks(t):
        t = jnp.pad(t, ((0, 0), (0, s_pad - S), (0, 0), (0, 0)))
        return t.reshape(B, nb, nk, dil, H, dh).transpose(0, 3, 4, 1, 2, 5)

    def with_prev(t):
        prev = jnp.pad(t, ((0, 0), (0, 0), (0, 0), (1, 0), (0, 0), (0, 0)))[:, :, :, :-1]
        return jnp.concatenate([prev, t], axis=4)

    qb = to_blocks(q)
    k2 = with_prev(to_blocks(k))
    v2 = with_prev(to_blocks(v))
    s = jnp.einsum("brhnid,brhnjd->brhnij", qb, k2,
                   preferred_element_type=jnp.float32) * (dh ** -0.5)
    i = jnp.arange(nk)[:, None]
    j = jnp.arange(2 * nk)[None, :]
    band = (j >= i) & (j <= i + nk)
    not_before_start = (jnp.arange(nb) > 0)[:, None, None] | (j >= nk)[None]
    mask = band[None] & not_before_start
    s = jnp.where(mask, s, -jnp.inf)
    lse = jax.nn.logsumexp(s, axis=-1)
    p = jnp.exp(s - lse[..., None]).astype(v.dtype)
    o = jnp.einsum("brhnij,brhnjd->brhnid", p, v2)
    o = o.transpose(0, 3, 4, 1, 2, 5).reshape(B, s_pad, H, dh)[:, :S]
    lse = lse.transpose(0, 3, 4, 1, 2).reshape(B, s_pad, H)[:, :S]
    return o, lse


def dilated_attn_sample(q, k_all, v_all, window, dil):
    Bd, T, H, dh = q.shape
    L = k_all.shape[1] - T
    nk = window // dil
    idx = (L + jnp.arange(T))[:, None] - dil * jnp.arange(nk + 1)[None, :]
    valid = idx >= 0
    idx = jnp.maximum(idx, 0)
    kg = k_all[:, idx]
    vg = v_all[:, idx]
    s = jnp.einsum("bthd,btkhd->bthk", q, kg,
                   preferred_element_type=jnp.float32) * (dh ** -0.5)
    s = jnp.where(valid[None, :, None, :], s, -jnp.inf)
    lse = jax.nn.logsumexp(s, axis=-1)
    p = jnp.exp(s - lse[..., None]).astype(v_all.dtype)
    o = jnp.einsum("bthk,btkhd->bthd", p, vg)
    return o, lse


def token_mix(h, caches, w_in, w_ba, w_bb, w_o, v_ln_g, v_ln_b, w_s, b_s):
    B, T, _ = h.shape
    proj = h @ w_in
    q, k, v, u, vb, ga, gb = jnp.split(proj, IN_SPLITS, axis=-1)
    shp = (B, T, N_HEADS_A, HEAD_DIM)
    q, k, v = q.reshape(shp), k.reshape(shp), v.reshape(shp)

    outs, lses, kv_states = [], [], []
    for g, (win, dil) in enumerate(DIL_GROUPS):
        sl = slice(g * HEADS_PER_GROUP, (g + 1) * HEADS_PER_GROUP)
        kv_new = jnp.stack([k[:, :, sl], v[:, :, sl]], axis=2)
        if caches is None:
            kv_all = kv_new
            o, lse = dilated_attn_prompt(q[:, :, sl], k[:, :, sl], v[:, :, sl], win, dil)
        else:
            kv_all = jnp.concatenate([caches[g], kv_new], axis=1)
            o, lse = dilated_attn_sample(q[:, :, sl], kv_all[:, :, 0], kv_all[:, :, 1], win, dil)
        kv_states.append(kv_all[:, kv_all.shape[1] - min(win, kv_all.shape[1]):])
        outs.append(o)
        lses.append(lse)
    wts = jax.nn.softmax(jnp.stack(lses, axis=0), axis=0)
    attn = (jnp.stack(outs, axis=0) * wts[..., None].astype(h.dtype))
    attn = attn.transpose(1, 2, 0, 3, 4).reshape(B, T, D_ATTN)

    vn = layer_norm(vb, v_ln_g, v_ln_b)
    ws = w_s * jnp.tril(jnp.ones((CHUNK, CHUNK), w_s.dtype))
    if caches is None:
        vc = vn.reshape(B, T // CHUNK, CHUNK, GMLP_GROUPS, GMLP_GROUP_DIM)
        mixed = jnp.einsum("gij,bcjgd->bcigd", ws, vc) + b_s.T[None, None, :, :, None]
        v_state = vn[:, T - CHUNK:]
    else:
        vc = vn.reshape(B, T, GMLP_GROUPS, GMLP_GROUP_DIM)
        mixed = jnp.einsum("gij,bjgd->bigd", ws[:, :T, :T], vc) + b_s[:, :T].T[None, :, :, None]
        v_state = vn
    gm = u * mixed.reshape(B, T, D_GMLP)

    merged = jax.nn.sigmoid(ga) * (attn @ w_ba) + jax.nn.sigmoid(gb) * (gm @ w_bb)
    return merged @ w_o, (kv_states[0], kv_states[1], kv_states[2], v_state)


def decoder_layer(x, c, caches, ada_w, ada_b, norm_g, ffn1_up, ffn1_down, w_in, w_ba, w_bb, w_o,
                  v_ln_g, v_ln_b, w_s, b_s, ffn2_up, ffn2_down):
    B = c.shape[0]
    mod = (jax.nn.silu(c) @ ada_w + ada_b).reshape(B, N_ADA, D_MODEL)[:, None]

    def pre(i, t):
        return rms_norm(t, norm_g[i]) * (1 + mod[:, :, 3 * i + 1]) + mod[:, :, 3 * i]

    x = x + 0.5 * mod[:, :, 2] * swiglu(pre(0, x), ffn1_up, ffn1_down)
    y, states = token_mix(pre(1, x), caches, w_in, w_ba, w_bb, w_o, v_ln_g, v_ln_b, w_s, b_s)
    x = x + mod[:, :, 5] * y
    x = x + 0.5 * mod[:, :, 8] * swiglu(pre(2, x), ffn2_up, ffn2_down)
    return x, states


def setup_inputs(seed: int = 0) -> dict:
    key = jax.random.key(seed)
    ks = jax.random.split(key, 24)
    f32 = jnp.float32

    def nrm(k, shape, scale):
        return jax.random.normal(k, shape, f32) * scale

    lens = [min(w, PAST_LEN) for w, _ in DIL_GROUPS]
    return {
        "x_prompt": nrm(ks[0], (BATCH, SEQ, D_MODEL), 1.0),
        "x_sample": nrm(ks[1], (DEC_BATCH, DEC_SEQ, D_MODEL), 1.0),
        "c_prompt": nrm(ks[2], (BATCH, D_MODEL), 1.0),
        "c_sample": nrm(ks[3], (DEC_BATCH, D_MODEL), 1.0),
        "cache_kv_g0": nrm(ks[4], (DEPTH, DEC_BATCH, lens[0], 2, HEADS_PER_GROUP, HEAD_DIM), 1.0),
        "cache_kv_g1": nrm(ks[5], (DEPTH, DEC_BATCH, lens[1], 2, HEADS_PER_GROUP, HEAD_DIM), 1.0),
        "cache_kv_g2": nrm(ks[6], (DEPTH, DEC_BATCH, lens[2], 2, HEADS_PER_GROUP, HEAD_DIM), 1.0),
        "ada_w": nrm(ks[7], (DEPTH, D_MODEL, N_ADA * D_MODEL), 0.5 * D_MODEL ** -0.5),
        "ada_b": nrm(ks[8], (DEPTH, N_ADA * D_MODEL), 0.02),
        "norm_g": 1.0 + nrm(ks[9], (DEPTH, 3, D_MODEL), 0.02),
        "ffn1_up": nrm(ks[10], (DEPTH, D_MODEL, 2 * D_FF), D_MODEL ** -0.5),
        "ffn1_down": nrm(ks[11], (DEPTH, D_FF, D_MODEL), D_FF ** -0.5),
        "w_in": nrm(ks[12], (DEPTH, D_MODEL, D_IN), D_MODEL ** -0.5),
        "w_branch_a": nrm(ks[13], (DEPTH, D_ATTN, D_MODEL), D_ATTN ** -0.5),
        "w_branch_b": nrm(ks[14], (DEPTH, D_GMLP, D_MODEL), D_GMLP ** -0.5),
        "w_out": nrm(ks[15], (DEPTH, D_MODEL, D_MODEL), D_MODEL ** -0.5),
        "v_ln_g": 1.0 + nrm(ks[16], (DEPTH, D_GMLP), 0.02),
        "v_ln_b": nrm(ks[17], (DEPTH, D_GMLP), 0.02),
        "w_spatial": nrm(ks[18], (DEPTH, GMLP_GROUPS, CHUNK, CHUNK), CHUNK ** -0.5),
        "b_spatial": 1.0 + nrm(ks[19], (DEPTH, GMLP_GROUPS, CHUNK), 0.02),
        "ffn2_up": nrm(ks[20], (DEPTH, D_MODEL, 2 * D_FF), D_MODEL ** -0.5),
        "ffn2_down": nrm(ks[21], (DEPTH, D_FF, D_MODEL), D_FF ** -0.5),
        "final_g": 1.0 + nrm(ks[22], (D_MODEL,), 0.02),
    }


def reference(x_prompt, x_sample, c_prompt, c_sample, cache_kv_g0, cache_kv_g1, cache_kv_g2,
              ada_w, ada_b, norm_g, ffn1_up, ffn1_down, w_in, w_branch_a, w_branch_b, w_out,
              v_ln_g, v_ln_b, w_spatial, b_spatial, ffn2_up, ffn2_down, final_g):
    hp, hs = x_prompt, x_sample
    st_p, st_s = [], []
    for l in range(DEPTH):
        weights = (ada_w[l], ada_b[l], norm_g[l], ffn1_up[l], ffn1_down[l], w_in[l], w_branch_a[l],
                   w_branch_b[l], w_out[l], v_ln_g[l], v_ln_b[l], w_spatial[l], b_spatial[l],
                   ffn2_up[l], ffn2_down[l])
        hp, sp = decoder_layer(hp, c_prompt, None, *weights)
        hs, ss = decoder_layer(hs, c_sample, (cache_kv_g0[l], cache_kv_g1[l], cache_kv_g2[l]), *weights)
        st_p.append(sp)
        st_s.append(ss)
    y_prompt = rms_norm(hp, final_g)
    y_sample = rms_norm(hs, final_g)
    kv_g0_prompt = jnp.stack([s[0] for s in st_p])
    kv_g1_prompt = jnp.stack([s[1] for s in st_p])
    kv_g2_prompt = jnp.stack([s[2] for s in st_p])
    vrows_prompt = jnp.stack([s[3] for s in st_p])
    kv_g0_sample = jnp.stack([s[0] for s in st_s])
    kv_g1_sample = jnp.stack([s[1] for s in st_s])
    kv_g2_sample = jnp.stack([s[2] for s in st_s])
    vrows_sample = jnp.stack([s[3] for s in st_s])
    return (y_prompt, y_sample, kv_g0_prompt, kv_g1_prompt, kv_g2_prompt, vrows_prompt,
            kv_g0_sample, kv_g1_sample, kv_g2_sample, vrows_sample)
```

```python
import numpy as np
import concourse.bass as bass
import concourse.mybir as mybir

F32 = mybir.dt.float32
BF16 = mybir.dt.bfloat16
AF = mybir.ActivationFunctionType
ALU = mybir.AluOpType
AX = mybir.AxisListType


class Buf:
    __slots__ = ("name", "w", "r", "excl")

    def __init__(self, name, excl=False):
        self.name = name
        self.excl = excl
        self.w = None
        self.r = {}


class Op:
    __slots__ = ("eng", "fn", "waits", "flag", "cnt", "dma", "dsem", "dval", "pos")


class Sched:
    ENGS = ("pe", "act", "dve", "pool", "sp")
    ND = 6

    def __init__(self):
        self.ops = {e: [] for e in self.ENGS}
        self.ndma = {e: 0 for e in self.ENGS}
        self.dma_ops = {e: [] for e in self.ENGS}

    def add(self, eng, fn, reads=(), writes=(), dma=False):
        op = Op()
        op.eng = eng
        op.fn = fn
        op.flag = False
        op.cnt = 0
        op.dma = dma
        op.dsem = None
        op.dval = 0
        op.pos = len(self.ops[eng])
        waits = []
        ex = [b for b in reads if b.excl]
        if ex:
            writes = list(writes) + [b for b in ex if b not in writes]
            reads = [b for b in reads if not b.excl]

        def need(p, kind):
            if p is op:
                return
            if p.dma:
                waits.append(p)
                return
            if p.eng == eng and not dma:
                if kind != "raw" or eng == "pe":
                    return
            p.flag = True
            waits.append(p)

        for b in reads:
            if b.w is not None:
                need(b.w, "raw")
        for b in writes:
            if b.w is not None:
                need(b.w, "waw")
            for k, r in b.r.items():
                if k == "dma":
                    for rr in r:
                        need(rr, "war")
                else:
                    need(r, "war")
        if dma:
            i = self.ndma[eng]
            op.dsem = i % self.ND
            op.dval = 16 * (i // self.ND + 1)
            if i >= self.ND:
                waits.append(self.dma_ops[eng][i - self.ND])
            self.ndma[eng] += 1
            self.dma_ops[eng].append(op)
        op.waits = waits
        for b in reads:
            if dma:
                b.r.setdefault("dma", []).append(op)
            else:
                b.r[eng] = op
        for b in writes:
            b.w = op
            b.r = {}
        self.ops[eng].append(op)
        return op

    def emit(self, nc):
        for e in self.ENGS:
            c = 0
            for op in self.ops[e]:
                if op.flag and not op.dma:
                    c += 1
                    op.cnt = c
        from contextlib import ExitStack
        with ExitStack() as es:
            csem = {e: es.enter_context(nc.semaphore("c_" + e)) for e in self.ENGS}
            dsem = {e: [es.enter_context(nc.semaphore("d_%s%d" % (e, i))) for i in range(self.ND)]
                    for e in self.ENGS if self.ndma[e] > 0}
            block = es.enter_context(nc.Block())
            sched = self

            def run(e, eng):
                seen = {}
                for op in sched.ops[e]:
                    req = {}
                    for p in op.waits:
                        if p.dma:
                            key = ("d", p.eng, p.dsem)
                            val = p.dval
                        else:
                            key = ("c", p.eng)
                            val = p.cnt
                        if req.get(key, 0) < val:
                            req[key] = val
                    for key, val in req.items():
                        if seen.get(key, 0) >= val:
                            continue
                        seen[key] = val
                        s = dsem[key[1]][key[2]] if key[0] == "d" else csem[key[1]]
                        eng.wait_ge(s, val)
                    ins = op.fn(eng)
                    if op.dma:
                        ins.then_inc(dsem[e][op.dsem], 16)
                    elif op.flag:
                        ins.then_inc(csem[e], 1)
                if sched.ndma[e] > 0:
                    last = {}
                    for op in sched.dma_ops[e]:
                        last[op.dsem] = op.dval
                    for k, v in last.items():
                        if seen.get(("d", e, k), 0) < v:
                            eng.wait_ge(dsem[e][k], v)

            @block.tensor
            def _(eng):
                run("pe", eng)

            @block.scalar
            def _(eng):
                run("act", eng)

            @block.vector
            def _(eng):
                run("dve", eng)

            @block.gpsimd
            def _(eng):
                run("pool", eng)

            @block.sync
            def _(eng):
                run("sp", eng)

import ml_dtypes
from contextlib import ExitStack
from concourse.bass_utils import run_bass_kernel_spmd

D = 1024
DFF = 2816
DIN = 5376
EPS = 1e-6
NSLOT = 4
NS = (2, 2, 5)
QOFF, UOFF, GOFF = 12, 18, 8


def build(n_seq=2, n_tiles=8, do_sample=True):
    nc = bass.Bass("TRN2", target_bir_lowering=False)
    S = Sched()

    def din(n, shp, dt=F32):
        return nc.dram_tensor(n, list(shp), dt, kind="ExternalInput").ap()

    def dout(n, shp):
        return nc.dram_tensor(n, list(shp), F32, kind="ExternalOutput").ap()

    def dint(n, shp, dt):
        return nc.dram_tensor(n, list(shp), dt, kind="Internal").ap()

    def sb(n, shp, dt):
        return nc.alloc_sbuf_tensor(n, list(shp), dt)

    import os as _os
    KSTOP = _os.environ.get("KSTOP", "")
    _stopflag = [False]

    def mark(name):
        if KSTOP and name == KSTOP:
            _stopflag[0] = True

    def _flat(l):
        o = []
        for b_ in l:
            if isinstance(b_, (list, tuple)):
                o.extend(_flat(b_))
            else:
                o.append(b_)
        return o

    def A(eng, meth, reads, writes, *args, **kw):
        if _stopflag[0]:
            return None
        return S.add(eng, (lambda e: getattr(e, meth)(*args, **kw)), _flat(reads), _flat(writes))

    def DMA(q, out, in_, reads, writes, **kw):
        if _stopflag[0]:
            return None
        return S.add(q, (lambda e: e.dma_start(out=out, in_=in_, **kw)), _flat(reads), _flat(writes), dma=True)

    xp = din("xp", [2, 4096, D]); xs = din("xs", [16, D]); cv = din("cv", [18, D])
    ck = [din("ck0", [16, 128, 512]), din("ck1", [16, 512, 512]), din("ck2", [16, 2048, 512])]
    ada_w = din("ada_w", [D, 9 * D]); ada_b = din("ada_b", [1, 9 * D]); norm_g = din("norm_g", [3, D])
    wshape = {"f1u": [D, 2 * DFF], "f1d": [DFF, D], "win": [D, DIN], "wba": [768, D], "wbb": [512, D],
              "wo": [D, D], "f2u": [D, 2 * DFF], "f2d": [DFF, D]}
    worder = ["f1u", "f1d", "win", "wba", "wbb", "wo", "f2u", "f2d"]
    wsrc = {k: din("w_" + k, wshape[k]) for k in worder}
    vlng = din("v_ln_g", [1, 512]); vlnb = din("v_ln_b", [1, 512])
    wsp = din("w_sp", [4, 128, 128]); bsp = din("b_sp", [4, 128]); fing = din("final_g", [1, D])
    c_identb = din("c_identb", [128, 128], BF16); c_identf = din("c_identf", [128, 128])
    c_m01 = din("c_m01", [128, 256], BF16); c_m2 = din("c_m2", [128, 640], BF16)
    c_sel = din("c_sel", [18, 2, 128]); c_E = din("c_E", [128, 16, 16], BF16)
    yp = dout("yp", [2, 4096, D]); ys = dout("ys", [16, D])
    kvp = [dout("kv0p", [2, 128, 512]), dout("kv1p", [2, 512, 512]), dout("kv2p", [2, 2048, 512])]
    vrp = dout("vrp", [2, 128, 512])
    kvs = [dout("kv0s", [16, 128, 512]), dout("kv1s", [16, 512, 512]), dout("kv2s", [16, 2048, 512])]
    vrs = dout("vrs", [16, 512])
    wb = {k: dint("b_" + k, wshape[k], BF16) for k in worder}
    wbB = {k: [] for k in worder}

    ring = sb("ring", [128, NSLOT, 5632], BF16); ringB = [[Buf("ring%d_%d" % (i, h)) for h in range(2)] for i in range(NSLOT)]
    x = sb("x", [128, 8, D], F32); xB = [Buf("x%d" % i) for i in range(8)]
    hT = sb("hT", [128, 8, 512], BF16); hTB = [Buf("hT%d" % i) for i in range(8)]
    big = sb("big", [128, 22, 512], BF16); bigB = [Buf("big%d" % i) for i in range(22)]
    ss = sb("ss", [128, 8], F32); ssB = [Buf("ss%d" % i) for i in range(8)]
    rs = sb("rs", [128, 8], F32); rsB = [Buf("rs%d" % i) for i in range(8)]
    ri = sb("ri", [128, 8], F32); riB = [Buf("ri%d" % i) for i in range(8)]
    sg = sb("sg", [128, 2, 512], BF16); sgB = [Buf("sg0"), Buf("sg1")]
    sga = sb("sga", [128, 2, 512], BF16); sgaB = [Buf("sga0"), Buf("sga1")]
    tg = sb("tg", [128, 2, 512], F32); tgB = [Buf("tg0"), Buf("tg1")]
    tA = sb("tA", [128, 2, 512], F32); tAB = [Buf("tA0"), Buf("tA1")]
    stg = sb("stg", [128, 2, 512], F32); stgB = [Buf("stg%d" % i) for i in range(2)]
    gsd = dint("gsd", [18, 3, D], F32); gsdB = Buf("gsd")
    gbc = sb("gbc", [128, 3, D], F32); gbcB = Buf("gbc")
    modT = sb("modT", [128, 6, 8, 18], F32); modB = Buf("modT")
    gT = sb("gT", [128, 3, 8], F32); gTB = Buf("gT")
    fg = sb("fg", [128, D], F32); fgB = Buf("fg")
    lng = sb("lng", [128, 512], F32); lnb = sb("lnb", [128, 512], F32); lnB = Buf("ln")
    bsbc = sb("bsbc", [128, 4, 128], F32); bsB = Buf("bs")
    wsT = sb("wsT", [128, 4, 128], BF16); wsTB = Buf("wsT")
    identb = sb("identb", [128, 128], BF16); identf = sb("identf", [128, 128], F32)
    m01 = sb("m01", [128, 256], BF16); m2 = sb("m2", [128, 640], BF16)
    sel = sb("sel", [18, 2, 128], F32); Eoh = sb("Eoh", [128, 16, 16], BF16)
    onesb = sb("onesb", [128, 128], BF16); onef = sb("onef", [1, 32], F32)
    cB = Buf("consts")
    bnst = sb("bnst", [128, 2, 6], F32); bnag = sb("bnag", [128, 2, 2], F32); lrs = sb("lrs", [128, 2, 2], F32)
    bnB = [Buf("bn0"), Buf("bn1")]; lrsB = [Buf("lrs0"), Buf("lrs1")]
    vnb = sb("vnb", [128, 2, 512], BF16); vnbB = [Buf("vnb0"), Buf("vnb1")]
    tmp16 = sb("tmp16", [128, 2, 16], F32); tmp16B = [Buf("t16a"), Buf("t16b")]

    pacc = nc.alloc_psum_tensor("pacc", [128, 4, 512], F32); paccB = [Buf("pacc%d" % i, True) for i in range(4)]
    pT = nc.alloc_psum_tensor("pT", [128, 4, 512], BF16); _pb0, _pb1 = Buf("pTb0", True), Buf("pTb1", True); pTB = [_pb0, _pb0, _pb1, _pb1]
    pS = nc.alloc_psum_tensor("pS", [128, 2, 512], F32); pSB = [Buf("pS0", True), Buf("pS1", True)]
    pTf = pT[:].bitcast(F32).rearrange("p a b -> p (a b)")
    pSf = pS[:].rearrange("p a b -> p (a b)")
    pAf = pacc[:, 2:4, :].rearrange("p a b -> p (a b)")
    cnt = {"pacc": 0, "ring": 0, "sg": 0, "sga": 0, "tg": 0, "tA": 0, "stg": 0, "bn": 0, "vnb": 0, "t16": 0, "pt": 0}

    def nxt(k, n):
        v = cnt[k] % n
        cnt[k] += 1
        if k == "pt":
            v = (0, 2, 1, 3)[v]
        return v

    def wblk(src, reads, dims, dt=BF16):
        s = nxt("ring", NSLOT)
        n = int(np.prod(dims))
        base = ring[:, s, :] if dt == BF16 else ring[:, s, :].bitcast(F32)
        flat = base[:, 0:n]
        if len(dims) == 2:
            v = flat.rearrange("p (a b) -> p a b", a=dims[0])
        else:
            v = flat.rearrange("p (a b c) -> p a b c", a=dims[0], b=dims[1])
        if len(dims) == 2:
            DMA("sp", v, src, reads, [ringB[s]])
        else:
            for ab in range(dims[1]):
                DMA("sp", v[:, :, ab, :], src[:, :, ab, :], reads, [ringB[s][ab]])
        return v, ringB[s], flat

    for dst, src in ((identb, c_identb), (identf, c_identf), (m01, c_m01), (m2, c_m2), (sel, c_sel), (Eoh, c_E)):
        DMA("sp", dst[:], src, [], [cB])
    A("dve", "memset", [], [cB], onesb[:], 1.0)
    A("dve", "memset", [], [cB], onef[:], 1.0)
    DMA("sp", fg[:], fing[0].partition_broadcast(128), [], [fgB])
    DMA("sp", lng[:], vlng[0].partition_broadcast(128), [], [lnB])
    DMA("sp", lnb[:], vlnb[0].partition_broadcast(128), [], [lnB])
    DMA("sp", bsbc[:], bsp.partition_broadcast(128), [], [bsB])
    for k in worder:
        R_, C_ = wshape[k]
        for r0 in range(0, R_, 128):
            for c0 in range(0, C_, 2048):
                c1 = min(C_, c0 + 2048)
                _b = Buf("wb_%s_%d_%d" % (k, r0, c0))
                wbB[k].append(_b)
                DMA("pool", wb[k][r0:r0 + 128, c0:c1], wsrc[k][r0:r0 + 128, c0:c1], [], [_b])

    esp = ExitStack()
    def sbp(n, shp, dt):
        return esp.enter_context(nc.sbuf_tensor(n, list(shp), dt))
    wsl = sbp("wsl", [128, 4, 128], F32); wslb = sbp("wslb", [128, 4, 128], BF16); wslB = Buf("wsl"); wslbB = Buf("wslb")
    DMA("sp", wsl[:], wsp.rearrange("g i j -> i g j"), [], [wslB])
    A("act", "copy", [wslB], [wslbB], out=wslb[:], in_=wsl[:])
    for g in range(4):
        A("pe", "transpose", [wslbB, cB], [pTB[0]], out=pT[:, 0, g * 128:(g + 1) * 128], in_=wslb[:, g, :], identity=identb[:])
    A("dve", "tensor_tensor", [pTB[0], cB], [wsTB], out=wsT[:], in0=pT[:, 0, :].rearrange("p (g i) -> p g i", g=4),
      in1=m01[:, 128:256].unsqueeze(1).broadcast_to([128, 4, 128]), op=ALU.mult)

    ng = sbp("ng", [3, D], F32); ngB = Buf("ng")
    DMA("sp", ng[:], norm_g, [], [ngB])
    pa = nxt("pacc", 4)
    for c in range(8):
        A("pe", "transpose", [ngB, cB], [paccB[pa]], out=pacc[:, pa, c * 3:(c + 1) * 3], in_=ng[:3, c * 128:(c + 1) * 128], identity=identf[:3, :3])
    A("dve", "tensor_copy", [paccB[pa]], [gTB], out=gT[:].rearrange("p w c -> p c w"), in_=pacc[:, pa, 0:24].rearrange("p (c w) -> p c w", w=3))

    cs = sbp("cs", [18, D], F32); cs2 = sbp("cs2", [18, D], F32); csB = Buf("cs"); cs2B = Buf("cs2")
    scT = sbp("scT", [128, 8, 18], F32); scTB = Buf("scT")
    bt = sbp("bt", [1, 2, 256], F32); btB = [Buf("bt0"), Buf("bt1")]
    mt = sbp("mt", [18, 2, 256], F32); mtB = [Buf("mt0"), Buf("mt1")]
    DMA("sp", cs[:], cv, [], [csB])
    A("act", "activation", [csB], [cs2B], out=cs2[:], in_=cs[:], func=AF.Silu)
    pa = nxt("pacc", 4)
    for kc in range(8):
        A("pe", "transpose", [cs2B, cB], [paccB[pa]], out=pacc[:, pa, kc * 18:(kc + 1) * 18], in_=cs2[:18, kc * 128:(kc + 1) * 128], identity=identf[:18, :18])
    A("dve", "tensor_copy", [paccB[pa]], [scTB], out=scT[:], in_=pacc[:, pa, 0:144].rearrange("p (k r) -> p k r", r=18))
    adav = ada_w.rearrange("(kc p) c -> p kc c", p=128)
    for t in range(36):
        i, off = divmod(t * 256, D)
        if (not do_sample) and False:
            pass
        wv, wB, _ = wblk(adav[:, :, t * 256:(t + 1) * 256], [], (8, 256), dt=F32)
        k = t % 2
        DMA("sp", bt[0:1, k, :], ada_b[0:1, t * 256:(t + 1) * 256], [], [btB[k]])
        pa = nxt("pacc", 4)
        for kc in range(8):
            A("pe", "matmul", [scTB, wB], [paccB[pa]], out=pacc[:18, pa, 0:256], lhsT=scT[:, kc, :], rhs=wv[:, kc, :], start=(kc == 0), stop=False)
        A("pe", "matmul", [cB, btB[k]], [paccB[pa]], out=pacc[:18, pa, 0:256], lhsT=onef[0:1, 0:18], rhs=bt[0:1, k, :], start=False, stop=True)
        if i in (2, 5, 8):
            A("act", "activation", [paccB[pa]], [mtB[k]], out=mt[:18, k, :], in_=pacc[:18, pa, 0:256], func=AF.Copy,
              scale=(1.0 if i == 5 else 0.5))
            DMA("sp", gsd[:, i // 3, off:off + 256], mt[:18, k, :], [mtB[k]], [gsdB])
        else:
            A("act", "copy", [paccB[pa]], [mtB[k]], out=mt[:18, k, :], in_=pacc[:18, pa, 0:256])
            which, kind = i // 3, i % 3
            pa2 = nxt("pacc", 4)
            for cc in range(2):
                A("pe", "transpose", [mtB[k], cB], [paccB[pa2]], out=pacc[:, pa2, cc * 18:(cc + 1) * 18], in_=mt[:18, k, cc * 128:(cc + 1) * 128], identity=identf[:18, :18])
            for cc in range(2):
                c = off // 128 + cc
                if kind == 1:
                    A("dve", "tensor_scalar", [paccB[pa2], gTB], [modB], out=modT[:, 2 * which + 1, c, :], in0=pacc[:, pa2, cc * 18:(cc + 1) * 18],
                      scalar1=1.0, scalar2=gT[:, which, c:c + 1], op0=ALU.add, op1=ALU.mult)
                else:
                    A("dve", "tensor_copy", [paccB[pa2]], [modB], out=modT[:, 2 * which, c, :], in_=pacc[:, pa2, cc * 18:(cc + 1) * 18])

    st = {"rows": 128, "nsub": 4, "TT": 512, "P": True, "seq": 0, "n": 0}

    def gate_ap(gi, c0, w):
        if st["P"]:
            return gbc[:, gi, c0:c0 + w], gbcB
        return st["gss"][0:16, gi, c0:c0 + w], gsdB2

    def mm_acc(pa, outw, pairs, reads, rows=128):
        n = len(pairs)
        for i, (l, r) in enumerate(pairs):
            A("pe", "matmul", reads[i], [paccB[pa]], out=pacc[:rows, pa, 0:outw], lhsT=l, rhs=r, start=(i == 0), stop=(i == n - 1))

    def XS(s):
        return 4 * st.get("par", 0) + s

    def xnv(s, rows):
        return big[:rows, 2 * s:2 * s + 2, :].rearrange("p a b -> p (a b)")

    def xnB_(s):
        return [bigB[2 * s], bigB[2 * s + 1]]

    def rstd_of(s):
        rows = st["rows"]
        i = XS(s)
        A("act", "activation", [xB[i]], [sgB, ssB[i]], out=sg[:rows].rearrange("p a b -> p (a b)"), in_=x[:rows, i, :], func=AF.Square,
          accum_out=ss[:rows, i:i + 1])
        A("act", "activation", [ssB[i]], [rsB[i]], out=rs[:rows, i:i + 1], in_=ss[:rows, i:i + 1], func=AF.Sqrt, scale=1.0 / D, bias=EPS)
        A("dve", "reciprocal", [rsB[i]], [riB[i]], out=ri[:rows, i:i + 1], in_=rs[:rows, i:i + 1])

    def prologue():
        rows, nsub = st["rows"], st["nsub"]
        for s in range(nsub):
            i = XS(s)
            rstd_of(s)
            if s % 2 == 0:
                A("act", "activation", [xB[i], riB[i]], [xnB_(s)], out=xnv(s, rows), in_=x[:rows, i, :], func=AF.Copy, scale=ri[:rows, i:i + 1])
            else:
                A("pool", "tensor_scalar", [xB[i], riB[i]], [xnB_(s)], out=xnv(s, rows), in0=x[:rows, i, :], scalar1=ri[:rows, i:i + 1],
                  scalar2=0.0, op0=ALU.mult, op1=ALU.add)

    def norm(which, do_pro=True):
        rows, nsub, TT = st["rows"], st["nsub"], st["TT"]
        if do_pro:
            prologue()
        for c in range(8):
            pb = nxt("pt", 4)
            for s in range(nsub):
                A("pe", "transpose", [xnB_(s), cB], [pTB[pb]], out=pT[:, pb, s * 128:s * 128 + rows], in_=xnv(s, rows)[:, c * 128:(c + 1) * 128],
                  identity=identb[:rows, :rows])
            if st["P"]:
                r0 = 16 + st["seq"]
                A("dve", "tensor_scalar", [pTB[pb], modB], [hTB[c]], out=hT[:, c, :], in0=pT[:, pb, :], scalar1=modT[:, 2 * which + 1, c, r0:r0 + 1],
                  scalar2=modT[:, 2 * which, c, r0:r0 + 1], op0=ALU.mult, op1=ALU.add)
            else:
                k = nxt("t16", 2)
                A("dve", "tensor_tensor", [pTB[pb], modB], [tmp16B[k]], out=tmp16[:, k, :], in0=pT[:, pb, 0:16], in1=modT[:, 2 * which + 1, c, 0:16], op=ALU.mult)
                A("dve", "tensor_tensor", [tmp16B[k], modB], [hTB[c]], out=hT[:, c, 0:16], in0=tmp16[:, k, :], in1=modT[:, 2 * which, c, 0:16], op=ALU.add)

    def resid(pa, s, c0, w, gi):
        rows = st["rows"]
        g, gB = gate_ap(gi, c0, w)
        k = nxt("tg", 2)
        A("dve", "tensor_tensor", [paccB[pa], gB], [tgB[k]], out=tg[:rows, k, 0:w], in0=pacc[:rows, pa, 0:w], in1=g[:rows], op=ALU.mult)
        i = XS(s)
        A("pool", "tensor_tensor", [tgB[k], xB[i]], [xB[i]], out=x[:rows, i, c0:c0 + w], in0=x[:rows, i, c0:c0 + w], in1=tg[:rows, k, 0:w], op=ALU.add)

    def ffn(wk, which, gi, do_pro=True):
        rows, nsub, TT = st["rows"], st["nsub"], st["TT"]
        norm(which, do_pro)
        mark("norm")
        upv = wb[wk + "u"].rearrange("(kc p) (ab n) -> p kc ab n", p=128, ab=2)
        for jb in range(11):
            wv, wB, _ = wblk(upv[:, :, :, jb * 256:(jb + 1) * 256], wbB[wk + "u"], (8, 2, 256))
            for jj in range(2):
                j = jb * 2 + jj
                pa = nxt("pacc", 4); pb = nxt("pacc", 4)
                mm_acc(pa, TT, [(wv[:, kc, 0, jj * 128:(jj + 1) * 128], hT[:, kc, 0:TT]) for kc in range(8)], [[wB, hTB[kc]] for kc in range(8)])
                mm_acc(pb, TT, [(wv[:, kc, 1, jj * 128:(jj + 1) * 128], hT[:, kc, 0:TT]) for kc in range(8)], [[wB, hTB[kc]] for kc in range(8)])
                k = nxt("sg", 2)
                A("act", "activation", [paccB[pa]], [sgB[k]], out=sg[:, k, 0:TT], in_=pacc[:, pa, 0:TT], func=AF.Silu)
                A("dve", "tensor_tensor", [sgB[k], paccB[pb]], [bigB[j]], out=big[:, j, 0:TT], in0=pacc[:, pb, 0:TT], in1=sg[:, k, 0:TT], op=ALU.mult)
        mark("up")
        dnv = wb[wk + "d"].rearrange("(j p) c -> p j c", p=128)
        for q in range(4):
            wv, wB, _ = wblk(dnv[:, :, q * 256:(q + 1) * 256], wbB[wk + "d"], (22, 256))
            for s in range(nsub):
                pa = nxt("pacc", 4)
                mm_acc(pa, 256, [(big[:, j, s * 128:s * 128 + rows], wv[:, j, :]) for j in range(22)], [[wB, bigB[j]] for j in range(22)], rows=rows)
                resid(pa, s, q * 256, 256, gi)
        mark("down")

    def final(dst_fn):
        rows, nsub = st["rows"], st["nsub"]
        for s in range(nsub):
            i = XS(s)
            rstd_of(s)
            A("dve", "scalar_tensor_tensor", [xB[i], riB[i], fgB], [xB[i]], out=x[:rows, i, :], in0=x[:rows, i, :], scalar=ri[:rows, i:i + 1],
              in1=fg[:rows, :], op0=ALU.mult, op1=ALU.mult)
            mark("fin_stt")
            DMA("act", dst_fn(s), x[:rows, i, :], [xB[i]], [])

    winv = wb["win"].rearrange("(kc p) c -> p kc c", p=128)

    def merged_and_out(attn_fn):
        rows, nsub, TT = st["rows"], st["nsub"], st["TT"]
        wbav = wb["wba"].rearrange("(kc p) c -> p kc c", p=128)
        wbbv = wb["wbb"].rearrange("(kc p) c -> p kc c", p=128)
        wov = wb["wo"].rearrange("(kc p) c -> p kc c", p=128)
        for cb in range(2):
            wga, wgaB, _ = wblk(winv[:, :, 3328 + cb * 512:3328 + (cb + 1) * 512], wbB["win"], (8, 512))
            wa, waB, _ = wblk(wbav[:, :, cb * 512:(cb + 1) * 512], wbB["wba"], (6, 512))
            wgb, wgbB, _ = wblk(winv[:, :, 4352 + cb * 512:4352 + (cb + 1) * 512], wbB["win"], (8, 512))
            wbv, wbvB, _ = wblk(wbbv[:, :, cb * 512:(cb + 1) * 512], wbB["wbb"], (4, 512))
            for cc in range(4):
                c = cb * 4 + cc
                cs_ = slice(cc * 128, (cc + 1) * 128)
                pg = nxt("pacc", 4)
                mm_acc(pg, TT, [(wga[:, kc, cs_], hT[:, kc, 0:TT]) for kc in range(8)], [[wgaB, hTB[kc]] for kc in range(8)])
                k1 = nxt("sga", 2)
                A("act", "activation", [paccB[pg]], [sgaB[k1]], out=sga[:, k1, 0:TT], in_=pacc[:, pg, 0:TT], func=AF.Sigmoid)
                pA = nxt("pacc", 4)
                prs = [attn_fn(kc) for kc in range(6)]
                mm_acc(pA, TT, [(wa[:, kc, cs_], prs[kc][0]) for kc in range(6)], [[waB, prs[kc][1]] for kc in range(6)])
                ka = nxt("tA", 2)
                A("dve", "tensor_tensor", [paccB[pA], sgaB[k1]], [tAB[ka]], out=tA[:, ka, 0:TT], in0=pacc[:, pA, 0:TT], in1=sga[:, k1, 0:TT], op=ALU.mult)
                pg2 = nxt("pacc", 4)
                mm_acc(pg2, TT, [(wgb[:, kc, cs_], hT[:, kc, 0:TT]) for kc in range(8)], [[wgbB, hTB[kc]] for kc in range(8)])
                k2 = nxt("sga", 2)
                A("act", "activation", [paccB[pg2]], [sgaB[k2]], out=sga[:, k2, 0:TT], in_=pacc[:, pg2, 0:TT], func=AF.Sigmoid)
                pBk = nxt("pacc", 4)
                mm_acc(pBk, TT, [(wbv[:, kc, cs_], big[:, GOFF + kc, 0:TT]) for kc in range(4)], [[wbvB, bigB[GOFF + kc]] for kc in range(4)])
                kb = nxt("tg", 2)
                A("dve", "tensor_tensor", [paccB[pBk], sgaB[k2]], [tgB[kb]], out=tg[:, kb, 0:TT], in0=pacc[:, pBk, 0:TT], in1=sga[:, k2, 0:TT], op=ALU.mult)
                A("pool", "tensor_tensor", [tAB[ka], tgB[kb]], [bigB[c]], out=big[:, c, 0:TT], in0=tA[:, ka, 0:TT], in1=tg[:, kb, 0:TT], op=ALU.add)
        mark("tm_merged")
        for hf in range(2):
            wv, wB, _ = wblk(wov[:, :, hf * 512:(hf + 1) * 512], wbB["wo"], (8, 512))
            for s in range(nsub):
                pa = nxt("pacc", 4)
                mm_acc(pa, 512, [(big[:, kc, s * 128:s * 128 + rows], wv[:, kc, :]) for kc in range(8)], [[wB, bigB[kc]] for kc in range(8)], rows=rows)
                resid(pa, s, hf * 512, 512, 1)

    def layernorm_rows(pa, rows):
        k = nxt("bn", 2)
        A("dve", "bn_stats", [paccB[pa]], [bnB[k]], out=bnst[:rows, k, :], in_=pacc[:rows, pa, :])
        A("dve", "bn_aggr", [bnB[k]], [bnB[k]], out=bnag[:rows, k, :], in_=bnst[:rows, k, :])
        A("act", "activation", [bnB[k]], [lrsB[k]], out=lrs[:rows, k, 0:1], in_=bnag[:rows, k, 1:2], func=AF.Sqrt, scale=1.0, bias=EPS)
        A("dve", "reciprocal", [lrsB[k]], [lrsB[k]], out=lrs[:rows, k, 1:2], in_=lrs[:rows, k, 0:1])
        ks = nxt("stg", 2)
        A("dve", "tensor_scalar", [paccB[pa], bnB[k], lrsB[k]], [stgB[ks]], out=stg[:rows, ks, :], in0=pacc[:rows, pa, :], scalar1=bnag[:rows, k, 0:1],
          scalar2=lrs[:rows, k, 1:2], op0=ALU.subtract, op1=ALU.mult)
        A("pool", "tensor_tensor", [stgB[ks], lnB], [stgB[ks]], out=stg[:rows, ks, :], in0=stg[:rows, ks, :], in1=lng[:rows, :], op=ALU.mult)
        A("pool", "tensor_tensor", [stgB[ks], lnB], [stgB[ks]], out=stg[:rows, ks, :], in0=stg[:rows, ks, :], in1=lnb[:rows, :], op=ALU.add)
        return ks

    samp_bufs = []
    deferred = []
    prep_bufs = [wslB, wslbB, ngB, csB, cs2B, scTB] + btB + mtB
    esp.close()
    gsdB2 = Buf("gss")
    with ExitStack() as es:
        def sbt(n, shp, dt):
            return es.enter_context(nc.sbuf_tensor(n, list(shp), dt))
        tok = sbt("tok", [16, 3328], F32); tokB = Buf("tok")
        kvt = sbt("kvt", [128, 4, 512], F32); kvtB = [Buf("kvt%d" % i) for i in range(4)]
        qbs = sbt("qbs", [128, 4, 256], F32); qbsB = [Buf("qbs%d" % i) for i in range(4)]
        prod = qbs; prodB = qbsB
        sc4 = sbt("sc4", [128, 4, 4], F32); sc4B = [Buf("sc4_%d" % i) for i in range(4)]
        pb4 = sbt("pb4", [128, 4, 4], BF16); pb4B = [Buf("pb4_%d" % i) for i in range(4)]
        pvt = sbt("pvt", [128, 4, 256], BF16); pvtB = [Buf("pvt%d" % i) for i in range(4)]
        sn12 = sbt("sn12", [16, 12], F32); pn12 = sbt("pn12", [16, 12], F32); snB = Buf("sn")
        osum = sbt("osum", [16, 768], F32); qk12 = osum; rsum = sbt("rsum", [16, 12], F32); osB = Buf("osum")
        rtot = sbt("rtot", [16, 4], F32); attn_s = sbt("attn_s", [16, 768], BF16); atsB = Buf("attn_s")
        attnTs = sbt("attnTs", [128, 6, 16], BF16); attnTsB = Buf("attnTs")
        gms = sbt("gms", [16, 512], F32); gmsb = sbt("gmsb", [16, 512], BF16); gmsB = Buf("gms")
        ws0 = sbt("ws0", [16, 4, 1], F32); bs0 = sbt("bs0", [16, 4, 1], F32); w0B = Buf("w0")
        gss = sbt("gss", [16, 3, D], F32); st["gss"] = gss
        bar0 = sbt("bar0", [128, 8], F32)
        samp_bufs = [gsdB2, tokB, snB, osB, atsB, attnTsB, gmsB, w0B] + kvtB + qbsB + prodB + sc4B + pb4B + pvtB

        A("dve", "memset", [], prep_bufs + samp_bufs, bar0[:], 0.0)
        if do_sample:
            st.update(rows=16, nsub=1, TT=16, P=False)
            DMA("sp", gss[:], gsd[0:16], [gsdB], [gsdB2])
            DMA("sp", x[:16, 0, :], xs, [], [xB[0]])
            DMA("sp", ws0[:], wsp[:, 0, 0:1].partition_broadcast(16), [], [w0B], allow_slow_non_contiguous=True)
            DMA("sp", bs0[:], bsp[:, 0:1].partition_broadcast(16), [], [w0B], allow_slow_non_contiguous=True)
            for g, L in enumerate((128, 512, 2048)):
                for b0 in range(16):
                    deferred.append((kvs[g][b0:b0 + 1, 0:L - 1, :], ck[g][b0:b0 + 1, 1:L, :]))
            ffn("f1", 0, 0)
            norm(1)
            for blk in range(7):
                c0 = blk * 512; w = min(512, 3328 - c0)
                wv, wB, _ = wblk(winv[:, :, c0:c0 + w], wbB["win"], (8, w))
                pa = nxt("pacc", 4)
                mm_acc(pa, w, [(hT[:, kc, 0:16], wv[:, kc, :]) for kc in range(8)], [[wB, hTB[kc]] for kc in range(8)], rows=16)
                A("act", "copy", [paccB[pa]], [tokB], out=tok[:, c0:c0 + w], in_=pacc[:16, pa, 0:w])
            for g, L in enumerate((128, 512, 2048)):
                DMA("sp", kvs[g][:, L - 1, 0:256], tok[:, 768 + g * 256:768 + (g + 1) * 256], [tokB], [])
                DMA("sp", kvs[g][:, L - 1, 256:512], tok[:, 1536 + g * 256:1536 + (g + 1) * 256], [tokB], [])
            A("dve", "tensor_tensor", [tokB], [snB], out=qk12[:], in0=tok[:, 0:768], in1=tok[:, 768:1536], op=ALU.mult)
            A("dve", "tensor_reduce", [snB], [snB], out=sn12[:], in_=qk12[:].rearrange("b (h d) -> b h d", d=64), axis=AX.X, op=ALU.add)
            A("act", "activation", [snB], [snB], out=pn12[:], in_=sn12[:], func=AF.Exp, scale=0.125)
            its = [(g, b) for g in range(3) for b in range(16)]

            def sp1(i):
                g, b = its[i]
                k = i % 4
                L = (128, 512, 2048)[g]
                dil = (1, 4, 16)[g]
                DMA("sp", kvt[:, k, :], ck[g][b, 0:L:dil, :], [], [kvtB[k]])
                pq = nxt("pacc", 4)
                A("pe", "matmul", [cB, tokB], [paccB[pq]], out=pacc[:, pq, 0:256], lhsT=identf[0:16, b:b + 1].broadcast_to([16, 128]),
                  rhs=tok[:, g * 256:(g + 1) * 256], start=True, stop=True)
                A("act", "copy", [paccB[pq]], [qbsB[k]], out=qbs[:, k, :], in_=pacc[:, pq, 0:256])
                A("dve", "tensor_tensor", [kvtB[k], qbsB[k]], [prodB[k]], out=prod[:, k, :], in0=kvt[:, k, 0:256], in1=qbs[:, k, :], op=ALU.mult)
                A("dve", "tensor_reduce", [prodB[k]], [sc4B[k]], out=sc4[:, k, :], in_=prod[:, k, :].rearrange("p (h d) -> p h d", d=64), axis=AX.X, op=ALU.add)
                A("act", "activation", [sc4B[k]], [pb4B[k]], out=pb4[:, k, :], in_=sc4[:, k, :], func=AF.Exp, scale=0.125)

            def sp2(i):
                g, b = its[i]
                k = i % 4
                A("dve", "tensor_tensor", [kvtB[k], pb4B[k]], [pvtB[k]], out=pvt[:, k, :].rearrange("p (h d) -> p h d", d=64),
                  in0=kvt[:, k, 256:512].rearrange("p (h d) -> p h d", d=64), in1=pb4[:, k, :].unsqueeze(2).broadcast_to([128, 4, 64]), op=ALU.mult)
                A("pe", "matmul", [cB, pvtB[k]], [pSB[0]], out=pS[:16, 0, 0:256], lhsT=Eoh[:, b, :], rhs=pvt[:, k, :], start=(b == 0), stop=(b == 15))
                A("pe", "matmul", [cB, pb4B[k]], [pSB[1]], out=pS[:16, 1, 0:4], lhsT=Eoh[:, b, :], rhs=pb4[:, k, :], start=(b == 0), stop=(b == 15))
                if b == 15:
                    A("dve", "tensor_tensor", [tokB, snB], [osB], out=osum[:, g * 256:(g + 1) * 256].rearrange("b (h d) -> b h d", d=64),
                      in0=tok[:, 1536 + g * 256:1536 + (g + 1) * 256].rearrange("b (h d) -> b h d", d=64),
                      in1=pn12[:, g * 4:(g + 1) * 4].unsqueeze(2).broadcast_to([16, 4, 64]), op=ALU.mult)
                    A("dve", "tensor_tensor", [osB, pSB[0]], [osB], out=osum[:, g * 256:(g + 1) * 256], in0=pS[:16, 0, 0:256], in1=osum[:, g * 256:(g + 1) * 256], op=ALU.add)
                    A("dve", "tensor_tensor", [snB, pSB[1]], [osB], out=rsum[:, g * 4:(g + 1) * 4], in0=pS[:16, 1, 0:4], in1=pn12[:, g * 4:(g + 1) * 4], op=ALU.add)

            for i in range(len(its) + 1):
                if i < len(its):
                    sp1(i)
                if i >= 1:
                    sp2(i - 1)
            A("dve", "tensor_tensor", [osB], [osB], out=rtot[:], in0=rsum[:, 0:4], in1=rsum[:, 4:8], op=ALU.add)
            A("dve", "tensor_tensor", [osB], [osB], out=rtot[:], in0=rtot[:], in1=rsum[:, 8:12], op=ALU.add)
            A("dve", "reciprocal", [osB], [osB], out=rtot[:], in_=rtot[:])
            A("dve", "tensor_tensor", [osB], [atsB], out=attn_s[:].rearrange("b (g h d) -> b g h d", g=3, h=4),
              in0=osum[:].rearrange("b (g h d) -> b g h d", g=3, h=4), in1=rtot[:].unsqueeze(1).unsqueeze(3).broadcast_to([16, 3, 4, 64]), op=ALU.mult)
            pb = nxt("pt", 4)
            for c in range(6):
                A("pe", "transpose", [atsB, cB], [pTB[pb]], out=pT[:, pb, c * 16:(c + 1) * 16], in_=attn_s[:, c * 128:(c + 1) * 128], identity=identb[:16, :16])
            A("act", "copy", [pTB[pb]], [attnTsB], out=attnTs[:], in_=pT[:, pb, 0:96].rearrange("p (c t) -> p c t", t=16))
            pa = nxt("pacc", 4)
            A("pe", "matmul", [cB, tokB], [paccB[pa]], out=pacc[:16, pa, :], lhsT=identf[0:16, 0:16], rhs=tok[:, 2816:3328], start=True, stop=True)
            ks = layernorm_rows(pa, 16)
            DMA("sp", vrs, stg[:16, ks, :], [stgB[ks]], [])
            A("dve", "tensor_tensor", [stgB[ks], w0B], [gmsB], out=gms[:].rearrange("b (g d) -> b g d", g=4), in0=stg[:16, ks, :].rearrange("b (g d) -> b g d", g=4), in1=ws0[:].broadcast_to([16, 4, 128]), op=ALU.mult)
            A("dve", "tensor_tensor", [gmsB, w0B], [gmsB], out=gms[:].rearrange("b (g d) -> b g d", g=4), in0=gms[:].rearrange("b (g d) -> b g d", g=4), in1=bs0[:].broadcast_to([16, 4, 128]), op=ALU.add)
            A("dve", "tensor_tensor", [gmsB, tokB], [gmsB], out=gmsb[:], in0=gms[:], in1=tok[:, 2304:2816], op=ALU.mult)
            pb = nxt("pt", 4)
            for c in range(4):
                A("pe", "transpose", [gmsB, cB], [pTB[pb]], out=pT[:, pb, c * 16:(c + 1) * 16], in_=gmsb[:, c * 128:(c + 1) * 128], identity=identb[:16, :16])
            A("act", "copy", [pTB[pb]], [bigB[GOFF + i] for i in range(4)], out=big[:, GOFF:GOFF + 4, 0:16], in_=pT[:, pb, 0:64].rearrange("p (c t) -> p c t", t=16))
            merged_and_out(lambda kc: (attnTs[:, kc, :], attnTsB))
            ffn("f2", 2, 2)
            final(lambda s: ys)

    kT = [sb("kT%d" % g, [128, 2, NS[g], 512], BF16) for g in range(3)]
    kTB = [[[Buf("kT%d_%d_%d" % (g, p, s)) for s in range(NS[g])] for p in range(2)] for g in range(3)]
    V = [sb("V%d" % g, [128, NS[g], 4, 256], BF16) for g in range(3)]
    VB = [[[Buf("V%d_%d_%d" % (g, s, u)) for u in range(4)] for s in range(NS[g])] for g in range(3)]
    Ot = sb("Ot", [128, 6, 512], BF16); OtB = [Buf("Ot%d" % i) for i in range(6)]
    RS = sb("RS", [128, 2, 512], F32); RSB = [Buf("RS0"), Buf("RS1")]
    PT = sb("PT", [128, 3, 640], BF16); PTB = [Buf("PT0"), Buf("PT1"), Buf("PT2")]
    print("sbuf bytes remaining", nc.sbuf_bytes_remaining)
    new_bufs = [b for g in kTB for p in g for b in p] + [b for g in VB for s in g for b in s] + OtB + RSB + PTB
    bar = sb("bar", [128, 8], F32)
    A("dve", "memset", [], samp_bufs + new_bufs, bar[:], 0.0)

    def tsel(ap, g, sub):
        if g == 0:
            return ap[:, sub * 128:(sub + 1) * 128]
        return ap[:, sub:512:4]

    def shp(ap, g):
        return ap

    def attention(seq, n):
        units = []
        for g in range(3):
            for p in range(2):
                for sub in range(4):
                    if g == 0:
                        pcs = []
                        if sub > 0:
                            pcs.append((n % 2, sub - 1))
                        elif n > 0:
                            pcs.append(((n - 1) % 2, 3))
                        pcs.append((n % 2, sub))
                        mask = m01[:, 256 - 128 * len(pcs):256]
                    elif g == 1:
                        pcs = ([((n - 1) % 2, sub)] if n > 0 else []) + [(n % 2, sub)]
                        mask = m01[:, 256 - 128 * len(pcs):256]
                    else:
                        pcs = [(m % 5, sub) for m in range(max(0, n - 4), n + 1)]
                        mask = m2[:, 640 - 128 * len(pcs):640]
                    for hh in range(2):
                        units.append((g, p, sub, hh, pcs, mask))
        SB3 = [(pSf, [pSB[0], pSB[1]]), (pTf, [pTB[0], pTB[2]]), (pAf, [paccB[2], paccB[3]])]

        def ph1(i):
            g, p, sub, hh, pcs, mask = units[i]
            c = g * 2 + p
            npc = len(pcs)
            k = i % 3
            psv, psb = SB3[k]
            hp = slice(64 * hh, 64 * hh + 64)
            q_ap = tsel(big[hp, QOFF + c, :], g, sub)
            for j, (sl, su) in enumerate(pcs):
                A("pe", "matmul", [kTB[g][p][sl], bigB[QOFF + c]], [psb[(j * 128) // 512]], out=psv[:, j * 128:(j + 1) * 128],
                  lhsT=tsel(kT[g][hp, p, sl, :], g, su), rhs=q_ap, start=True, stop=True)
            nb = (npc * 128 + 511) // 512
            A("act", "activation", psb[0:nb], [PTB[k]], out=PT[:, k, 0:npc * 128], in_=psv[:, 0:npc * 128], func=AF.Exp, scale=0.125)
            A("pool", "tensor_tensor", [PTB[k], cB], [PTB[k]], out=PT[:, k, 0:npc * 128], in0=PT[:, k, 0:npc * 128], in1=mask, op=ALU.mult)

        def ph2(i):
            g, p, sub, hh, pcs, mask = units[i]
            c = g * 2 + p
            npc = len(pcs)
            k = i % 3
            po = i % 2
            hp = slice(64 * hh, 64 * hh + 64)
            for j, (sl, su) in enumerate(pcs):
                A("pe", "matmul", [VB[g][sl][su], PTB[k]], [paccB[po]], out=pacc[:, po, 0:128], lhsT=V[g][:, sl, su, p * 128:(p + 1) * 128],
                  rhs=PT[:, k, j * 128:(j + 1) * 128], start=(j == 0), stop=(j == npc - 1))
            for j in range(npc):
                A("pe", "matmul", [cB, PTB[k]], [paccB[po]], out=pacc[:, po, 128:256], lhsT=onesb[:], rhs=PT[:, k, j * 128:(j + 1) * 128],
                  start=(j == 0), stop=(j == npc - 1))
            A("act", "copy", [paccB[po]], [OtB[c]], out=tsel(Ot[hp, c, :], g, sub), in_=pacc[hp, po, 0:128])
            if g == 0:
                A("dve", "tensor_copy", [paccB[po]], [RSB[p]], out=tsel(RS[hp, p, :], g, sub), in_=pacc[hp, po, 128:256])
            else:
                A("dve", "tensor_tensor", [paccB[po], RSB[p]], [RSB[p]], out=tsel(RS[hp, p, :], g, sub), in0=pacc[hp, po, 128:256],
                  in1=tsel(RS[hp, p, :], g, sub), op=ALU.add)

        DEP = 2
        for i in range(len(units) + DEP):
            if i < len(units):
                ph1(i)
            if i - DEP >= 0:
                ph2(i - DEP)
        for p in range(2):
            A("dve", "reciprocal", [RSB[p]], [RSB[p]], out=RS[:, p, :], in_=RS[:, p, :])
        for c in range(6):
            A("pool", "tensor_tensor", [OtB[c], RSB[c % 2]], [OtB[c]], out=Ot[:, c, :], in0=Ot[:, c, :], in1=RS[:, c % 2, :], op=ALU.mult)

    def tokmix_p(seq, n):
        norm(1)
        for blk in range(3):
            wv, wB, _ = wblk(winv[:, :, blk * 512:(blk + 1) * 512], wbB["win"], (8, 512))
            for cc in range(4):
                c = blk * 4 + cc
                pa = nxt("pacc", 4)
                mm_acc(pa, 512, [(wv[:, kc, cc * 128:(cc + 1) * 128], hT[:, kc, :]) for kc in range(8)], [[wB, hTB[kc]] for kc in range(8)])
                if c < 6:
                    A("act", "copy", [paccB[pa]], [bigB[QOFF + c]], out=big[:, QOFF + c, :], in_=pacc[:, pa, :])
                else:
                    g, p = divmod(c - 6, 2)
                    A("act", "copy", [paccB[pa]], [kTB[g][p][n % NS[g]]], out=kT[g][:, p, n % NS[g], :], in_=pacc[:, pa, :])
        mark("tm_qk")
        kvv = wb["win"][:, 768:2304].rearrange("(kc p) (ab m) -> p kc ab m", p=128, ab=2)
        for g in range(3):
            wv, wB, flat = wblk(kvv[:, :, :, g * 256:(g + 1) * 256], wbB["win"], (8, 2, 256))
            w2 = flat.rearrange("p (kc m) -> p kc m", kc=8)
            for sub in range(4):
                pa = nxt("pacc", 4)
                want = (g == 0 and n == 7 and sub == 3) or (g == 1 and n == 7) or (g == 2 and n >= 4)
                sl = n % NS[g]
                if want:
                    mm_acc(pa, 512, [(tsel(hT[:, kc, :], g, sub), w2[:, kc, :]) for kc in range(8)], [[wB, hTB[kc]] for kc in range(8)])
                    A("act", "copy", [paccB[pa]], [VB[g][sl][sub]], out=V[g][:, sl, sub, :], in_=pacc[:, pa, 256:512])
                else:
                    mm_acc(pa, 256, [(tsel(hT[:, kc, :], g, sub), w2[:, kc, 256:512]) for kc in range(8)], [[wB, hTB[kc]] for kc in range(8)])
                    A("act", "copy", [paccB[pa]], [VB[g][sl][sub]], out=V[g][:, sl, sub, :], in_=pacc[:, pa, 0:256])
                if want:
                    ks = nxt("stg", 2)
                    A("dve", "tensor_copy", [paccB[pa]], [stgB[ks]], out=stg[:, ks, :], in_=pacc[:, pa, :])
                    if g == 0:
                        DMA("act", kvp[0][seq], stg[:, ks, :], [stgB[ks]], [])
                    elif g == 1:
                        DMA("act", kvp[1][seq, sub:512:4, :], stg[:, ks, :], [stgB[ks]], [])
                    else:
                        r0 = (n - 4) * 512 + sub
                        DMA("act", kvp[2][seq, r0:(n - 3) * 512:4, :], stg[:, ks, :], [stgB[ks]], [])
        mark("tm_kv")
        wv, wB, _ = wblk(winv[:, :, 2304:2816], wbB["win"], (8, 512))
        for cc in range(4):
            pa = nxt("pacc", 4)
            mm_acc(pa, 512, [(wv[:, kc, cc * 128:(cc + 1) * 128], hT[:, kc, :]) for kc in range(8)], [[wB, hTB[kc]] for kc in range(8)])
            A("act", "copy", [paccB[pa]], [bigB[UOFF + cc]], out=big[:, UOFF + cc, :], in_=pacc[:, pa, :])
        mark("tm_u")
        wv, wB, _ = wblk(winv[:, :, 2816:3328], wbB["win"], (8, 512))
        for s in range(4):
            pa = nxt("pacc", 4)
            mm_acc(pa, 512, [(hT[:, kc, s * 128:(s + 1) * 128], wv[:, kc, :]) for kc in range(8)], [[wB, hTB[kc]] for kc in range(8)])
            ks = layernorm_rows(pa, 128)
            if n == 7 and s == 3:
                DMA("act", vrp[seq], stg[:, ks, :], [stgB[ks]], [])
            kv_ = nxt("vnb", 2)
            A("act", "copy", [stgB[ks]], [vnbB[kv_]], out=vnb[:, kv_, :], in_=stg[:, ks, :])
            pa2 = nxt("pacc", 4)
            for g in range(4):
                A("pe", "matmul", [vnbB[kv_], wsTB], [paccB[pa2]], out=pacc[:, pa2, g * 128:(g + 1) * 128], lhsT=vnb[:, kv_, g * 128:(g + 1) * 128],
                  rhs=wsT[:, g, :], start=True, stop=True)
            k = nxt("tg", 2)
            A("dve", "tensor_tensor", [paccB[pa2], bsB], [tgB[k]], out=tg[:, k, :], in0=pacc[:, pa2, :], in1=bsbc[:].rearrange("p g i -> p (g i)"), op=ALU.add)
            A("pool", "tensor_tensor", [tgB[k]] + [bigB[UOFF + i] for i in range(4)], [bigB[GOFF + i] for i in range(4)],
              out=big[:, GOFF:GOFF + 4, s * 128:(s + 1) * 128], in0=tg[:, k, :].rearrange("p (g i) -> p g i", g=4),
              in1=big[:, UOFF:UOFF + 4, s * 128:(s + 1) * 128], op=ALU.mult)
        mark("tm_vb")
        if not _os.environ.get("SKIP_ATT"):
            attention(seq, n)
        mark("tm_att")
        merged_and_out(lambda kc: (Ot[:, kc, :], OtB[kc]))

    st.update(rows=128, nsub=4, TT=512, P=True, par=0)
    tiles = [(seq, n) for seq in range(n_seq) for n in range(n_tiles)]

    def load_x(t):
        seq, n = tiles[t]
        for s in range(4):
            i = 4 * (t % 2) + s
            DMA("sp", x[:, i, :], xp[seq, n * 512 + s * 128:n * 512 + (s + 1) * 128, :], [], [xB[i]])

    if tiles:
        load_x(0)
        st["par"] = 0
        prologue()
    for t, (seq, n) in enumerate(tiles):
        st["seq"] = seq
        st["n"] = n
        st["par"] = t % 2
        if n == 0:
            for gi in range(3):
                DMA("sp", gbc[:, gi, :], gsd[16 + seq, gi, :].partition_broadcast(128), [gsdB], [gbcB])
        for _ in range(3):
            if deferred:
                o_, i_ = deferred.pop(0)
                DMA("pool", o_, i_, [], [])
        ffn("f1", 0, 0, do_pro=False)
        if t + 1 < len(tiles):
            load_x(t + 1)
        tokmix_p(seq, n)
        mark("tm_wo")
        ffn("f2", 2, 2)
        mark("ffn2")
        if t + 1 < len(tiles):
            st["par"] = (t + 1) % 2
            prologue()
            st["par"] = t % 2
        final(lambda s: yp[seq, n * 512 + s * 128:n * 512 + (s + 1) * 128, :])
    while deferred:
        o_, i_ = deferred.pop(0)
        DMA("pool", o_, i_, [], [])
    S.emit(nc)
    return nc


_NC = {}


def _consts():
    bf = ml_dtypes.bfloat16
    j = np.arange(128)[:, None]; i = np.arange(128)[None, :]
    M0 = (j >= i); M1 = (j <= i)
    BD = (j % 4 == i % 4)
    BM0 = BD & ((j // 4) >= (i // 4)); BM1 = BD & ((j // 4) <= (i // 4))
    sel = np.zeros((18, 2, 128), np.float32); sel[16, 0, :] = 1; sel[17, 1, :] = 1
    E = np.zeros((128, 16, 16), np.float32)
    for b in range(16):
        E[:, b, b] = 1
    return {
        "c_identb": np.eye(128, dtype=np.float32).astype(bf), "c_identf": np.eye(128, dtype=np.float32),
        "c_m01": np.concatenate([M0, M1], 1).astype(np.float32).astype(bf),
        "c_m2": np.concatenate([BM0, BD, BD, BD, BM1], 1).astype(np.float32).astype(bf),
        "c_sel": sel, "c_E": E.astype(bf),
    }


def kernel(x_prompt, x_sample, c_prompt, c_sample, cache_kv_g0, cache_kv_g1, cache_kv_g2,
           ada_w, ada_b, norm_g, ffn1_up, ffn1_down, w_in, w_branch_a, w_branch_b, w_out,
           v_ln_g, v_ln_b, w_spatial, b_spatial, ffn2_up, ffn2_down, final_g):
    f = lambda a: np.ascontiguousarray(np.asarray(a, dtype=np.float32))
    if "nc" not in _NC:
        _NC["nc"] = build()
    nc = _NC["nc"]
    shared = {
        "ada_w": f(ada_w[0]), "ada_b": f(ada_b[0]).reshape(1, -1), "norm_g": f(norm_g[0]),
        "w_f1u": f(ffn1_up[0]), "w_f1d": f(ffn1_down[0]), "w_win": f(w_in[0]), "w_wba": f(w_branch_a[0]),
        "w_wbb": f(w_branch_b[0]), "w_wo": f(w_out[0]), "w_f2u": f(ffn2_up[0]), "w_f2d": f(ffn2_down[0]),
        "v_ln_g": f(v_ln_g[0]).reshape(1, -1), "v_ln_b": f(v_ln_b[0]).reshape(1, -1),
        "w_sp": f(w_spatial[0]), "b_sp": f(b_spatial[0]), "final_g": f(final_g).reshape(1, -1),
    }
    shared.update(_consts())
    x_prompt = np.asarray(x_prompt); x_sample = np.asarray(x_sample)
    in_maps = []
    for c in range(8):
        m = dict(shared)
        m["xp"] = f(x_prompt[2 * c:2 * c + 2])
        m["xs"] = f(x_sample[16 * c:16 * c + 16, 0])
        m["cv"] = f(np.concatenate([np.asarray(c_sample)[16 * c:16 * c + 16], np.asarray(c_prompt)[2 * c:2 * c + 2]], 0))
        m["ck0"] = f(np.asarray(cache_kv_g0)[0, 16 * c:16 * c + 16]).reshape(16, 128, 512)
        m["ck1"] = f(np.asarray(cache_kv_g1)[0, 16 * c:16 * c + 16]).reshape(16, 512, 512)
        m["ck2"] = f(np.asarray(cache_kv_g2)[0, 16 * c:16 * c + 16]).reshape(16, 2048, 512)
        in_maps.append(m)
    res = run_bass_kernel_spmd(nc, in_maps, core_ids=list(range(8)))
    R = res.results
    cat = lambda k: np.concatenate([np.asarray(r[k]) for r in R], 0)
    y_prompt = cat("yp")
    y_sample = cat("ys").reshape(128, 1, D)
    outs = [y_prompt, y_sample]
    for g, L in enumerate((128, 512, 2048)):
        outs.append(cat("kv%dp" % g).reshape(1, 16, L, 2, 4, 64))
    outs.append(cat("vrp").reshape(1, 16, 128, 512))
    for g, L in enumerate((128, 512, 2048)):
        outs.append(cat("kv%ds" % g).reshape(1, 128, L, 2, 4, 64))
    outs.append(cat("vrs").reshape(1, 128, 1, 512))
    return tuple(np.ascontiguousarray(o, dtype=np.float32) for o in outs)
```

```python
import numpy as np
import concourse.bass as bass
import concourse.mybir as mybir

F32 = mybir.dt.float32
BF16 = mybir.dt.bfloat16
AF = mybir.ActivationFunctionType
ALU = mybir.AluOpType
AX = mybir.AxisListType


class Buf:
    __slots__ = ("name", "w", "r", "excl")

    def __init__(self, name, excl=False):
        self.name = name
        self.excl = excl
        self.w = None
        self.r = {}


class Op:
    __slots__ = ("eng", "fn", "waits", "flag", "cnt", "dma", "dsem", "dval", "pos")


class Sched:
    ENGS = ("pe", "act", "dve", "pool", "sp")
    ND = 6

    def __init__(self):
        self.ops = {e: [] for e in self.ENGS}
        self.ndma = {e: 0 for e in self.ENGS}
        self.dma_ops = {e: [] for e in self.ENGS}

    def add(self, eng, fn, reads=(), writes=(), dma=False):
        op = Op()
        op.eng = eng
        op.fn = fn
        op.flag = False
        op.cnt = 0
        op.dma = dma
        op.dsem = None
        op.dval = 0
        op.pos = len(self.ops[eng])
        waits = []
        ex = [b for b in reads if b.excl]
        if ex:
            writes = list(writes) + [b for b in ex if b not in writes]
            reads = [b for b in reads if not b.excl]

        def need(p, kind):
            if p is op:
                return
            if p.dma:
                waits.append(p)
                return
            if p.eng == eng and not dma:
                if kind != "raw" or eng == "pe":
                    return
            p.flag = True
            waits.append(p)

        for b in reads:
            if b.w is not None:
                need(b.w, "raw")
        for b in writes:
            if b.w is not None:
                need(b.w, "waw")
            for k, r in b.r.items():
                if k == "dma":
                    for rr in r:
                        need(rr, "war")
                else:
                    need(r, "war")
        if dma:
            i = self.ndma[eng]
            op.dsem = i % self.ND
            op.dval = 16 * (i // self.ND + 1)
            if i >= self.ND:
                waits.append(self.dma_ops[eng][i - self.ND])
            self.ndma[eng] += 1
            self.dma_ops[eng].append(op)
        op.waits = waits
        for b in reads:
            if dma:
                b.r.setdefault("dma", []).append(op)
            else:
                b.r[eng] = op
        for b in writes:
            b.w = op
            b.r = {}
        self.ops[eng].append(op)
        return op

    def emit(self, nc):
        for e in self.ENGS:
            c = 0
            for op in self.ops[e]:
                if op.flag and not op.dma:
                    c += 1
                    op.cnt = c
        from contextlib import ExitStack
        with ExitStack() as es:
            csem = {e: es.enter_context(nc.semaphore("c_" + e)) for e in self.ENGS}
            dsem = {e: [es.enter_context(nc.semaphore("d_%s%d" % (e, i))) for i in range(self.ND)]
                    for e in self.ENGS if self.ndma[e] > 0}
            block = es.enter_context(nc.Block())
            sched = self

            def run(e, eng):
                seen = {}
                for op in sched.ops[e]:
                    req = {}
                    for p in op.waits:
                        if p.dma:
                            key = ("d", p.eng, p.dsem)
                            val = p.dval
                        else:
                            key = ("c", p.eng)
                            val = p.cnt
                        if req.get(key, 0) < val:
                            req[key] = val
                    for key, val in req.items():
                        if seen.get(key, 0) >= val:
                            continue
                        seen[key] = val
                        s = dsem[key[1]][key[2]] if key[0] == "d" else csem[key[1]]
                        eng.wait_ge(s, val)
                    ins = op.fn(eng)
                    if op.dma:
                        ins.then_inc(dsem[e][op.dsem], 16)
                    elif op.flag:
                        ins.then_inc(csem[e], 1)
                if sched.ndma[e] > 0:
                    last = {}
                    for op in sched.dma_ops[e]:
                        last[op.dsem] = op.dval
                    for k, v in last.items():
                        if seen.get(("d", e, k), 0) < v:
                            eng.wait_ge(dsem[e][k], v)

            @block.tensor
            def _(eng):
                run("pe", eng)

            @block.scalar
            def _(eng):
                run("act", eng)

            @block.vector
            def _(eng):
                run("dve", eng)

            @block.gpsimd
            def _(eng):
                run("pool", eng)

            @block.sync
            def _(eng):
                run("sp", eng)

import ml_dtypes
from contextlib import ExitStack
from concourse.bass_utils import run_bass_kernel_spmd

D = 1024
DFF = 2816
DIN = 5376
EPS = 1e-6
NSLOT = 4
NS = (2, 2, 5)
QOFF, UOFF, GOFF = 12, 18, 8


def build(n_seq=2, n_tiles=8, do_sample=True):
    nc = bass.Bass("TRN2", target_bir_lowering=False)
    S = Sched()

    def din(n, shp, dt=F32):
        return nc.dram_tensor(n, list(shp), dt, kind="ExternalInput").ap()

    def dout(n, shp):
        return nc.dram_tensor(n, list(shp), F32, kind="ExternalOutput").ap()

    def dint(n, shp, dt):
        return nc.dram_tensor(n, list(shp), dt, kind="Internal").ap()

    def sb(n, shp, dt):
        return nc.alloc_sbuf_tensor(n, list(shp), dt)

    import os as _os
    KSTOP = _os.environ.get("KSTOP", "")
    _stopflag = [False]

    def mark(name):
        if KSTOP and name == KSTOP:
            _stopflag[0] = True

    def _flat(l):
        o = []
        for b_ in l:
            if isinstance(b_, (list, tuple)):
                o.extend(_flat(b_))
            else:
                o.append(b_)
        return o

    def A(eng, meth, reads, writes, *args, **kw):
        if _stopflag[0]:
            return None
        return S.add(eng, (lambda e: getattr(e, meth)(*args, **kw)), _flat(reads), _flat(writes))

    def DMA(q, out, in_, reads, writes, **kw):
        if _stopflag[0]:
            return None
        return S.add(q, (lambda e: e.dma_start(out=out, in_=in_, **kw)), _flat(reads), _flat(writes), dma=True)

    xp = din("xp", [2, 4096, D]); xs = din("xs", [16, D]); cv = din("cv", [18, D])
    ck = [din("ck0", [16, 128, 512]), din("ck1", [16, 512, 512]), din("ck2", [16, 2048, 512])]
    ada_w = din("ada_w", [D, 9 * D]); ada_b = din("ada_b", [1, 9 * D]); norm_g = din("norm_g", [3, D])
    wshape = {"f1u": [D, 2 * DFF], "f1d": [DFF, D], "win": [D, DIN], "wba": [768, D], "wbb": [512, D],
              "wo": [D, D], "f2u": [D, 2 * DFF], "f2d": [DFF, D]}
    worder = ["f1u", "f1d", "win", "wba", "wbb", "wo", "f2u", "f2d"]
    wsrc = {k: din("w_" + k, wshape[k]) for k in worder}
    vlng = din("v_ln_g", [1, 512]); vlnb = din("v_ln_b", [1, 512])
    wsp = din("w_sp", [4, 128, 128]); bsp = din("b_sp", [4, 128]); fing = din("final_g", [1, D])
    c_identb = din("c_identb", [128, 128], BF16); c_identf = din("c_identf", [128, 128])
    c_m01 = din("c_m01", [128, 256], BF16); c_m2 = din("c_m2", [128, 640], BF16)
    c_sel = din("c_sel", [18, 2, 128]); c_E = din("c_E", [128, 16, 16], BF16)
    yp = dout("yp", [2, 4096, D]); ys = dout("ys", [16, D])
    kvp = [dout("kv0p", [2, 128, 512]), dout("kv1p", [2, 512, 512]), dout("kv2p", [2, 2048, 512])]
    vrp = dout("vrp", [2, 128, 512])
    kvs = [dout("kv0s", [16, 128, 512]), dout("kv1s", [16, 512, 512]), dout("kv2s", [16, 2048, 512])]
    vrs = dout("vrs", [16, 512])
    wb = {k: dint("b_" + k, wshape[k], BF16) for k in worder}
    wbB = {k: [] for k in worder}

    ring = sb("ring", [128, NSLOT, 5632], BF16); ringB = [[Buf("ring%d_%d" % (i, h)) for h in range(2)] for i in range(NSLOT)]
    x = sb("x", [128, 8, D], F32); xB = [Buf("x%d" % i) for i in range(8)]
    hT = sb("hT", [128, 8, 512], BF16); hTB = [Buf("hT%d" % i) for i in range(8)]
    big = sb("big", [128, 22, 512], BF16); bigB = [Buf("big%d" % i) for i in range(22)]
    ss = sb("ss", [128, 8], F32); ssB = [Buf("ss%d" % i) for i in range(8)]
    rs = sb("rs", [128, 8], F32); rsB = [Buf("rs%d" % i) for i in range(8)]
    ri = sb("ri", [128, 8], F32); riB = [Buf("ri%d" % i) for i in range(8)]
    sg = sb("sg", [128, 2, 512], BF16); sgB = [Buf("sg0"), Buf("sg1")]
    sga = sb("sga", [128, 2, 512], BF16); sgaB = [Buf("sga0"), Buf("sga1")]
    tg = sb("tg", [128, 2, 512], F32); tgB = [Buf("tg0"), Buf("tg1")]
    tA = sb("tA", [128, 2, 512], F32); tAB = [Buf("tA0"), Buf("tA1")]
    stg = sb("stg", [128, 2, 512], F32); stgB = [Buf("stg%d" % i) for i in range(2)]
    gsd = dint("gsd", [18, 3, D], F32); gsdB = Buf("gsd")
    gbc = sb("gbc", [128, 3, D], F32); gbcB = Buf("gbc")
    modT = sb("modT", [128, 6, 8, 18], F32); modB = Buf("modT")
    gT = sb("gT", [128, 3, 8], F32); gTB = Buf("gT")
    fg = sb("fg", [128, D], F32); fgB = Buf("fg")
    lng = sb("lng", [128, 512], F32); lnb = sb("lnb", [128, 512], F32); lnB = Buf("ln")
    bsbc = sb("bsbc", [128, 4, 128], F32); bsB = Buf("bs")
    wsT = sb("wsT", [128, 4, 128], BF16); wsTB = Buf("wsT")
    identb = sb("identb", [128, 128], BF16); identf = sb("identf", [128, 128], F32)
    m01 = sb("m01", [128, 256], BF16); m2 = sb("m2", [128, 640], BF16)
    sel = sb("sel", [18, 2, 128], F32); Eoh = sb("Eoh", [128, 16, 16], BF16)
    onesb = sb("onesb", [128, 128], BF16); onef = sb("onef", [1, 32], F32)
    cB = Buf("consts")
    bnst = sb("bnst", [128, 2, 6], F32); bnag = sb("bnag", [128, 2, 2], F32); lrs = sb("lrs", [128, 2, 2], F32)
    bnB = [Buf("bn0"), Buf("bn1")]; lrsB = [Buf("lrs0"), Buf("lrs1")]
    vnb = sb("vnb", [128, 2, 512], BF16); vnbB = [Buf("vnb0"), Buf("vnb1")]
    tmp16 = sb("tmp16", [128, 2, 16], F32); tmp16B = [Buf("t16a"), Buf("t16b")]

    pacc = nc.alloc_psum_tensor("pacc", [128, 4, 512], F32); paccB = [Buf("pacc%d" % i, True) for i in range(4)]
    pT = nc.alloc_psum_tensor("pT", [128, 4, 512], BF16); _pb0, _pb1 = Buf("pTb0", True), Buf("pTb1", True); pTB = [_pb0, _pb0, _pb1, _pb1]
    pS = nc.alloc_psum_tensor("pS", [128, 2, 512], F32); pSB = [Buf("pS0", True), Buf("pS1", True)]
    pTf = pT[:].bitcast(F32).rearrange("p a b -> p (a b)")
    pSf = pS[:].rearrange("p a b -> p (a b)")
    pAf = pacc[:, 2:4, :].rearrange("p a b -> p (a b)")
    cnt = {"pacc": 0, "ring": 0, "sg": 0, "sga": 0, "tg": 0, "tA": 0, "stg": 0, "bn": 0, "vnb": 0, "t16": 0, "pt": 0}

    def nxt(k, n):
        v = cnt[k] % n
        cnt[k] += 1
        if k == "pt":
            v = (0, 2, 1, 3)[v]
        return v

    def wblk(src, reads, dims, dt=BF16):
        s = nxt("ring", NSLOT)
        n = int(np.prod(dims))
        base = ring[:, s, :] if dt == BF16 else ring[:, s, :].bitcast(F32)
        flat = base[:, 0:n]
        if len(dims) == 2:
            v = flat.rearrange("p (a b) -> p a b", a=dims[0])
        else:
            v = flat.rearrange("p (a b c) -> p a b c", a=dims[0], b=dims[1])
        if len(dims) == 2:
            DMA("sp", v, src, reads, [ringB[s]])
        else:
            for ab in range(dims[1]):
                DMA("sp", v[:, :, ab, :], src[:, :, ab, :], reads, [ringB[s][ab]])
        return v, ringB[s], flat

    for dst, src in ((identb, c_identb), (identf, c_identf), (m01, c_m01), (m2, c_m2), (sel, c_sel), (Eoh, c_E)):
        DMA("sp", dst[:], src, [], [cB])
    A("dve", "memset", [], [cB], onesb[:], 1.0)
    A("dve", "memset", [], [cB], onef[:], 1.0)
    DMA("sp", fg[:], fing[0].partition_broadcast(128), [], [fgB])
    DMA("sp", lng[:], vlng[0].partition_broadcast(128), [], [lnB])
    DMA("sp", lnb[:], vlnb[0].partition_broadcast(128), [], [lnB])
    DMA("sp", bsbc[:], bsp.partition_broadcast(128), [], [bsB])
    for k in worder:
        R_, C_ = wshape[k]
        for r0 in range(0, R_, 128):
            for c0 in range(0, C_, 2048):
                c1 = min(C_, c0 + 2048)
                _b = Buf("wb_%s_%d_%d" % (k, r0, c0))
                wbB[k].append(_b)
                DMA("pool", wb[k][r0:r0 + 128, c0:c1], wsrc[k][r0:r0 + 128, c0:c1], [], [_b])

    esp = ExitStack()
    def sbp(n, shp, dt):
        return esp.enter_context(nc.sbuf_tensor(n, list(shp), dt))
    wsl = sbp("wsl", [128, 4, 128], F32); wslb = sbp("wslb", [128, 4, 128], BF16); wslB = Buf("wsl"); wslbB = Buf("wslb")
    DMA("sp", wsl[:], wsp.rearrange("g i j -> i g j"), [], [wslB])
    A("act", "copy", [wslB], [wslbB], out=wslb[:], in_=wsl[:])
    for g in range(4):
        A("pe", "transpose", [wslbB, cB], [pTB[0]], out=pT[:, 0, g * 128:(g + 1) * 128], in_=wslb[:, g, :], identity=identb[:])
    A("dve", "tensor_tensor", [pTB[0], cB], [wsTB], out=wsT[:], in0=pT[:, 0, :].rearrange("p (g i) -> p g i", g=4),
      in1=m01[:, 128:256].unsqueeze(1).broadcast_to([128, 4, 128]), op=ALU.mult)

    ng = sbp("ng", [3, D], F32); ngB = Buf("ng")
    DMA("sp", ng[:], norm_g, [], [ngB])
    pa = nxt("pacc", 4)
    for c in range(8):
        A("pe", "transpose", [ngB, cB], [paccB[pa]], out=pacc[:, pa, c * 3:(c + 1) * 3], in_=ng[:3, c * 128:(c + 1) * 128], identity=identf[:3, :3])
    A("dve", "tensor_copy", [paccB[pa]], [gTB], out=gT[:].rearrange("p w c -> p c w"), in_=pacc[:, pa, 0:24].rearrange("p (c w) -> p c w", w=3))

    cs = sbp("cs", [18, D], F32); cs2 = sbp("cs2", [18, D], F32); csB = Buf("cs"); cs2B = Buf("cs2")
    scT = sbp("scT", [128, 8, 18], F32); scTB = Buf("scT")
    bt = sbp("bt", [1, 2, 256], F32); btB = [Buf("bt0"), Buf("bt1")]
    mt = sbp("mt", [18, 2, 256], F32); mtB = [Buf("mt0"), Buf("mt1")]
    DMA("sp", cs[:], cv, [], [csB])
    A("act", "activation", [csB], [cs2B], out=cs2[:], in_=cs[:], func=AF.Silu)
    pa = nxt("pacc", 4)
    for kc in range(8):
        A("pe", "transpose", [cs2B, cB], [paccB[pa]], out=pacc[:, pa, kc * 18:(kc + 1) * 18], in_=cs2[:18, kc * 128:(kc + 1) * 128], identity=identf[:18, :18])
    A("dve", "tensor_copy", [paccB[pa]], [scTB], out=scT[:], in_=pacc[:, pa, 0:144].rearrange("p (k r) -> p k r", r=18))
    adav = ada_w.rearrange("(kc p) c -> p kc c", p=128)
    for t in range(36):
        i, off = divmod(t * 256, D)
        if (not do_sample) and False:
            pass
        wv, wB, _ = wblk(adav[:, :, t * 256:(t + 1) * 256], [], (8, 256), dt=F32)
        k = t % 2
        DMA("sp", bt[0:1, k, :], ada_b[0:1, t * 256:(t + 1) * 256], [], [btB[k]])
        pa = nxt("pacc", 4)
        for kc in range(8):
            A("pe", "matmul", [scTB, wB], [paccB[pa]], out=pacc[:18, pa, 0:256], lhsT=scT[:, kc, :], rhs=wv[:, kc, :], start=(kc == 0), stop=False)
        A("pe", "matmul", [cB, btB[k]], [paccB[pa]], out=pacc[:18, pa, 0:256], lhsT=onef[0:1, 0:18], rhs=bt[0:1, k, :], start=False, stop=True)
        if i in (2, 5, 8):
            A("act", "activation", [paccB[pa]], [mtB[k]], out=mt[:18, k, :], in_=pacc[:18, pa, 0:256], func=AF.Copy,
              scale=(1.0 if i == 5 else 0.5))
            DMA("sp", gsd[:, i // 3, off:off + 256], mt[:18, k, :], [mtB[k]], [gsdB])
        else:
            A("act", "copy", [paccB[pa]], [mtB[k]], out=mt[:18, k, :], in_=pacc[:18, pa, 0:256])
            which, kind = i // 3, i % 3
            pa2 = nxt("pacc", 4)
            for cc in range(2):
                A("pe", "transpose", [mtB[k], cB], [paccB[pa2]], out=pacc[:, pa2, cc * 18:(cc + 1) * 18], in_=mt[:18, k, cc * 128:(cc + 1) * 128], identity=identf[:18, :18])
            for cc in range(2):
                c = off // 128 + cc
                if kind == 1:
                    A("dve", "tensor_scalar", [paccB[pa2], gTB], [modB], out=modT[:, 2 * which + 1, c, :], in0=pacc[:, pa2, cc * 18:(cc + 1) * 18],
                      scalar1=1.0, scalar2=gT[:, which, c:c + 1], op0=ALU.add, op1=ALU.mult)
                else:
                    A("dve", "tensor_copy", [paccB[pa2]], [modB], out=modT[:, 2 * which, c, :], in_=pacc[:, pa2, cc * 18:(cc + 1) * 18])

    st = {"rows": 128, "nsub": 4, "TT": 512, "P": True, "seq": 0, "n": 0}

    def gate_ap(gi, c0, w):
        if st["P"]:
            return gbc[:, gi, c0:c0 + w], gbcB
        return st["gss"][0:16, gi, c0:c0 + w], gsdB2

    def mm_acc(pa, outw, pairs, reads, rows=128):
        n = len(pairs)
        for i, (l, r) in enumerate(pairs):
            A("pe", "matmul", reads[i], [paccB[pa]], out=pacc[:rows, pa, 0:outw], lhsT=l, rhs=r, start=(i == 0), stop=(i == n - 1))

    def XS(s):
        return 4 * st.get("par", 0) + s

    def xnv(s, rows):
        return big[:rows, 2 * s:2 * s + 2, :].rearrange("p a b -> p (a b)")

    def xnB_(s):
        return [bigB[2 * s], bigB[2 * s + 1]]

    def sq_of(s):
        rows = st["rows"]
        i = XS(s)
        if s % 2 == 0:
            A("act", "activation", [xB[i]], [sgB, ssB[i]], out=sg[:rows].rearrange("p a b -> p (a b)"), in_=x[:rows, i, :], func=AF.Square,
              accum_out=ss[:rows, i:i + 1])
        else:
            A("dve", "scalar_tensor_tensor", [xB[i]], [sgaB, ssB[i]], out=sga[:rows].rearrange("p a b -> p (a b)"), in0=x[:rows, i, :], scalar=1.0,
              in1=x[:rows, i, :], op0=ALU.mult, op1=ALU.mult, accum_out=ss[:rows, i:i + 1])

    def rs_of(s):
        rows = st["rows"]
        i = XS(s)
        A("act", "activation", [ssB[i]], [rsB[i]], out=rs[:rows, i:i + 1], in_=ss[:rows, i:i + 1], func=AF.Sqrt, scale=1.0 / D, bias=EPS)
        A("dve", "reciprocal", [rsB[i]], [riB[i]], out=ri[:rows, i:i + 1], in_=rs[:rows, i:i + 1])

    def prologue():
        rows, nsub = st["rows"], st["nsub"]
        for s in range(nsub):
            sq_of(s)
        for s in range(nsub):
            rs_of(s)
        for s in range(nsub):
            i = XS(s)
            if s % 2 == 0:
                A("act", "activation", [xB[i], riB[i]], [xnB_(s)], out=xnv(s, rows), in_=x[:rows, i, :], func=AF.Copy, scale=ri[:rows, i:i + 1])
            else:
                A("pool", "tensor_scalar", [xB[i], riB[i]], [xnB_(s)], out=xnv(s, rows), in0=x[:rows, i, :], scalar1=ri[:rows, i:i + 1],
                  scalar2=0.0, op0=ALU.mult, op1=ALU.add)

    def norm(which, do_pro=True):
        rows, nsub, TT = st["rows"], st["nsub"], st["TT"]
        if do_pro:
            prologue()
        for c in range(8):
            pb = nxt("pt", 4)
            for s in range(nsub):
                A("pe", "transpose", [xnB_(s), cB], [pTB[pb]], out=pT[:, pb, s * 128:s * 128 + rows], in_=xnv(s, rows)[:, c * 128:(c + 1) * 128],
                  identity=identb[:rows, :rows])
            if st["P"]:
                r0 = 16 + st["seq"]
                A("dve", "tensor_scalar", [pTB[pb], modB], [hTB[c]], out=hT[:, c, :], in0=pT[:, pb, :], scalar1=modT[:, 2 * which + 1, c, r0:r0 + 1],
                  scalar2=modT[:, 2 * which, c, r0:r0 + 1], op0=ALU.mult, op1=ALU.add)
            else:
                k = nxt("t16", 2)
                A("dve", "tensor_tensor", [pTB[pb], modB], [tmp16B[k]], out=tmp16[:, k, :], in0=pT[:, pb, 0:16], in1=modT[:, 2 * which + 1, c, 0:16], op=ALU.mult)
                A("dve", "tensor_tensor", [tmp16B[k], modB], [hTB[c]], out=hT[:, c, 0:16], in0=tmp16[:, k, :], in1=modT[:, 2 * which, c, 0:16], op=ALU.add)

    def resid(pa, s, c0, w, gi):
        rows = st["rows"]
        g, gB = gate_ap(gi, c0, w)
        k = nxt("tg", 2)
        A("dve", "tensor_tensor", [paccB[pa], gB], [tgB[k]], out=tg[:rows, k, 0:w], in0=pacc[:rows, pa, 0:w], in1=g[:rows], op=ALU.mult)
        i = XS(s)
        A("pool", "tensor_tensor", [tgB[k], xB[i]], [xB[i]], out=x[:rows, i, c0:c0 + w], in0=x[:rows, i, c0:c0 + w], in1=tg[:rows, k, 0:w], op=ALU.add)

    def ffn(wk, which, gi, do_pro=True):
        rows, nsub, TT = st["rows"], st["nsub"], st["TT"]
        norm(which, do_pro)
        mark("norm")
        upv = wb[wk + "u"].rearrange("(kc p) (ab n) -> p kc ab n", p=128, ab=2)
        for jb in range(11):
            wv, wB, _ = wblk(upv[:, :, :, jb * 256:(jb + 1) * 256], wbB[wk + "u"], (8, 2, 256))
            for jj in range(2):
                j = jb * 2 + jj
                pa = nxt("pacc", 4); pb = nxt("pacc", 4)
                mm_acc(pa, TT, [(wv[:, kc, 0, jj * 128:(jj + 1) * 128], hT[:, kc, 0:TT]) for kc in range(8)], [[wB, hTB[kc]] for kc in range(8)])
                mm_acc(pb, TT, [(wv[:, kc, 1, jj * 128:(jj + 1) * 128], hT[:, kc, 0:TT]) for kc in range(8)], [[wB, hTB[kc]] for kc in range(8)])
                k = nxt("sg", 2)
                A("act", "activation", [paccB[pa]], [sgB[k]], out=sg[:, k, 0:TT], in_=pacc[:, pa, 0:TT], func=AF.Silu)
                A("dve", "tensor_tensor", [sgB[k], paccB[pb]], [bigB[j]], out=big[:, j, 0:TT], in0=pacc[:, pb, 0:TT], in1=sg[:, k, 0:TT], op=ALU.mult)
        mark("up")
        dnv = wb[wk + "d"].rearrange("(j p) c -> p j c", p=128)
        for q in range(4):
            wv, wB, _ = wblk(dnv[:, :, q * 256:(q + 1) * 256], wbB[wk + "d"], (22, 256))
            for s in range(nsub):
                pa = nxt("pacc", 4)
                mm_acc(pa, 256, [(big[:, j, s * 128:s * 128 + rows], wv[:, j, :]) for j in range(22)], [[wB, bigB[j]] for j in range(22)], rows=rows)
                resid(pa, s, q * 256, 256, gi)
        mark("down")

    def final(dst_fn):
        rows, nsub = st["rows"], st["nsub"]
        for s in range(nsub):
            sq_of(s)
        for s in range(nsub):
            rs_of(s)
        for s in range(nsub):
            i = XS(s)
            A("dve", "scalar_tensor_tensor", [xB[i], riB[i], fgB], [xB[i]], out=x[:rows, i, :], in0=x[:rows, i, :], scalar=ri[:rows, i:i + 1],
              in1=fg[:rows, :], op0=ALU.mult, op1=ALU.mult)
            mark("fin_stt")
            DMA("act", dst_fn(s), x[:rows, i, :], [xB[i]], [])

    winv = wb["win"].rearrange("(kc p) c -> p kc c", p=128)

    def merged_and_out(attn_fn):
        rows, nsub, TT = st["rows"], st["nsub"], st["TT"]
        wbav = wb["wba"].rearrange("(kc p) c -> p kc c", p=128)
        wbbv = wb["wbb"].rearrange("(kc p) c -> p kc c", p=128)
        wov = wb["wo"].rearrange("(kc p) c -> p kc c", p=128)
        for cb in range(2):
            wga, wgaB, _ = wblk(winv[:, :, 3328 + cb * 512:3328 + (cb + 1) * 512], wbB["win"], (8, 512))
            wa, waB, _ = wblk(wbav[:, :, cb * 512:(cb + 1) * 512], wbB["wba"], (6, 512))
            wgb, wgbB, _ = wblk(winv[:, :, 4352 + cb * 512:4352 + (cb + 1) * 512], wbB["win"], (8, 512))
            wbv, wbvB, _ = wblk(wbbv[:, :, cb * 512:(cb + 1) * 512], wbB["wbb"], (4, 512))
            for cc in range(4):
                c = cb * 4 + cc
                cs_ = slice(cc * 128, (cc + 1) * 128)
                pg = nxt("pacc", 4)
                mm_acc(pg, TT, [(wga[:, kc, cs_], hT[:, kc, 0:TT]) for kc in range(8)], [[wgaB, hTB[kc]] for kc in range(8)])
                k1 = nxt("sga", 2)
                A("act", "activation", [paccB[pg]], [sgaB[k1]], out=sga[:, k1, 0:TT], in_=pacc[:, pg, 0:TT], func=AF.Sigmoid)
                pA = nxt("pacc", 4)
                prs = [attn_fn(kc) for kc in range(6)]
                mm_acc(pA, TT, [(wa[:, kc, cs_], prs[kc][0]) for kc in range(6)], [[waB, prs[kc][1]] for kc in range(6)])
                ka = nxt("tA", 2)
                A("dve", "tensor_tensor", [paccB[pA], sgaB[k1]], [tAB[ka]], out=tA[:, ka, 0:TT], in0=pacc[:, pA, 0:TT], in1=sga[:, k1, 0:TT], op=ALU.mult)
                pg2 = nxt("pacc", 4)
                mm_acc(pg2, TT, [(wgb[:, kc, cs_], hT[:, kc, 0:TT]) for kc in range(8)], [[wgbB, hTB[kc]] for kc in range(8)])
                k2 = nxt("sga", 2)
                A("act", "activation", [paccB[pg2]], [sgaB[k2]], out=sga[:, k2, 0:TT], in_=pacc[:, pg2, 0:TT], func=AF.Sigmoid)
                pBk = nxt("pacc", 4)
                mm_acc(pBk, TT, [(wbv[:, kc, cs_], big[:, GOFF + kc, 0:TT]) for kc in range(4)], [[wbvB, bigB[GOFF + kc]] for kc in range(4)])
                kb = nxt("tg", 2)
                A("dve", "tensor_tensor", [paccB[pBk], sgaB[k2]], [tgB[kb]], out=tg[:, kb, 0:TT], in0=pacc[:, pBk, 0:TT], in1=sga[:, k2, 0:TT], op=ALU.mult)
                A("pool", "tensor_tensor", [tAB[ka], tgB[kb]], [bigB[c]], out=big[:, c, 0:TT], in0=tA[:, ka, 0:TT], in1=tg[:, kb, 0:TT], op=ALU.add)
        mark("tm_merged")
        for hf in range(2):
            wv, wB, _ = wblk(wov[:, :, hf * 512:(hf + 1) * 512], wbB["wo"], (8, 512))
            for s in range(nsub):
                pa = nxt("pacc", 4)
                mm_acc(pa, 512, [(big[:, kc, s * 128:s * 128 + rows], wv[:, kc, :]) for kc in range(8)], [[wB, bigB[kc]] for kc in range(8)], rows=rows)
                resid(pa, s, hf * 512, 512, 1)

    def layernorm_rows(pa, rows):
        k = nxt("bn", 2)
        A("dve", "bn_stats", [paccB[pa]], [bnB[k]], out=bnst[:rows, k, :], in_=pacc[:rows, pa, :])
        A("dve", "bn_aggr", [bnB[k]], [bnB[k]], out=bnag[:rows, k, :], in_=bnst[:rows, k, :])
        A("act", "activation", [bnB[k]], [lrsB[k]], out=lrs[:rows, k, 0:1], in_=bnag[:rows, k, 1:2], func=AF.Sqrt, scale=1.0, bias=EPS)
        A("dve", "reciprocal", [lrsB[k]], [lrsB[k]], out=lrs[:rows, k, 1:2], in_=lrs[:rows, k, 0:1])
        ks = nxt("stg", 2)
        A("dve", "tensor_scalar", [paccB[pa], bnB[k], lrsB[k]], [stgB[ks]], out=stg[:rows, ks, :], in0=pacc[:rows, pa, :], scalar1=bnag[:rows, k, 0:1],
          scalar2=lrs[:rows, k, 1:2], op0=ALU.subtract, op1=ALU.mult)
        A("pool", "tensor_tensor", [stgB[ks], lnB], [stgB[ks]], out=stg[:rows, ks, :], in0=stg[:rows, ks, :], in1=lng[:rows, :], op=ALU.mult)
        A("pool", "tensor_tensor", [stgB[ks], lnB], [stgB[ks]], out=stg[:rows, ks, :], in0=stg[:rows, ks, :], in1=lnb[:rows, :], op=ALU.add)
        return ks

    samp_bufs = []
    deferred = []
    prep_bufs = [wslB, wslbB, ngB, csB, cs2B, scTB] + btB + mtB
    esp.close()
    gsdB2 = Buf("gss")
    with ExitStack() as es:
        def sbt(n, shp, dt):
            return es.enter_context(nc.sbuf_tensor(n, list(shp), dt))
        tok = sbt("tok", [16, 3328], F32); tokB = Buf("tok")
        kvt = sbt("kvt", [128, 4, 512], F32); kvtB = [Buf("kvt%d" % i) for i in range(4)]
        qbs = sbt("qbs", [128, 4, 256], F32); qbsB = [Buf("qbs%d" % i) for i in range(4)]
        prod = qbs; prodB = qbsB
        sc4 = sbt("sc4", [128, 4, 4], F32); sc4B = [Buf("sc4_%d" % i) for i in range(4)]
        pb4 = sbt("pb4", [128, 4, 4], BF16); pb4B = [Buf("pb4_%d" % i) for i in range(4)]
        pvt = sbt("pvt", [128, 4, 256], BF16); pvtB = [Buf("pvt%d" % i) for i in range(4)]
        sn12 = sbt("sn12", [16, 12], F32); pn12 = sbt("pn12", [16, 12], F32); snB = Buf("sn")
        osum = sbt("osum", [16, 768], F32); qk12 = osum; rsum = sbt("rsum", [16, 12], F32); osB = Buf("osum")
        rtot = sbt("rtot", [16, 4], F32); attn_s = sbt("attn_s", [16, 768], BF16); atsB = Buf("attn_s")
        attnTs = sbt("attnTs", [128, 6, 16], BF16); attnTsB = Buf("attnTs")
        gms = sbt("gms", [16, 512], F32); gmsb = sbt("gmsb", [16, 512], BF16); gmsB = Buf("gms")
        ws0 = sbt("ws0", [16, 4, 1], F32); bs0 = sbt("bs0", [16, 4, 1], F32); w0B = Buf("w0")
        gss = sbt("gss", [16, 3, D], F32); st["gss"] = gss
        bar0 = sbt("bar0", [128, 8], F32)
        samp_bufs = [gsdB2, tokB, snB, osB, atsB, attnTsB, gmsB, w0B] + kvtB + qbsB + prodB + sc4B + pb4B + pvtB

        A("dve", "memset", [], prep_bufs + samp_bufs, bar0[:], 0.0)
        if do_sample:
            st.update(rows=16, nsub=1, TT=16, P=False)
            DMA("sp", gss[:], gsd[0:16], [gsdB], [gsdB2])
            DMA("sp", x[:16, 0, :], xs, [], [xB[0]])
            DMA("sp", ws0[:], wsp[:, 0, 0:1].partition_broadcast(16), [], [w0B], allow_slow_non_contiguous=True)
            DMA("sp", bs0[:], bsp[:, 0:1].partition_broadcast(16), [], [w0B], allow_slow_non_contiguous=True)
            for g, L in enumerate((128, 512, 2048)):
                for b0 in range(16):
                    deferred.append((kvs[g][b0:b0 + 1, 0:L - 1, :], ck[g][b0:b0 + 1, 1:L, :]))
            ffn("f1", 0, 0)
            norm(1)
            for blk in range(7):
                c0 = blk * 512; w = min(512, 3328 - c0)
                wv, wB, _ = wblk(winv[:, :, c0:c0 + w], wbB["win"], (8, w))
                pa = nxt("pacc", 4)
                mm_acc(pa, w, [(hT[:, kc, 0:16], wv[:, kc, :]) for kc in range(8)], [[wB, hTB[kc]] for kc in range(8)], rows=16)
                A("act", "copy", [paccB[pa]], [tokB], out=tok[:, c0:c0 + w], in_=pacc[:16, pa, 0:w])
            for g, L in enumerate((128, 512, 2048)):
                DMA("sp", kvs[g][:, L - 1, 0:256], tok[:, 768 + g * 256:768 + (g + 1) * 256], [tokB], [])
                DMA("sp", kvs[g][:, L - 1, 256:512], tok[:, 1536 + g * 256:1536 + (g + 1) * 256], [tokB], [])
            A("dve", "tensor_tensor", [tokB], [snB], out=qk12[:], in0=tok[:, 0:768], in1=tok[:, 768:1536], op=ALU.mult)
            A("dve", "tensor_reduce", [snB], [snB], out=sn12[:], in_=qk12[:].rearrange("b (h d) -> b h d", d=64), axis=AX.X, op=ALU.add)
            A("act", "activation", [snB], [snB], out=pn12[:], in_=sn12[:], func=AF.Exp, scale=0.125)
            its = [(g, b) for g in range(3) for b in range(16)]

            def sp1(i):
                g, b = its[i]
                k = i % 4
                L = (128, 512, 2048)[g]
                dil = (1, 4, 16)[g]
                DMA("sp", kvt[:, k, :], ck[g][b, 0:L:dil, :], [], [kvtB[k]])
                pq = nxt("pacc", 4)
                A("pe", "matmul", [cB, tokB], [paccB[pq]], out=pacc[:, pq, 0:256], lhsT=identf[0:16, b:b + 1].broadcast_to([16, 128]),
                  rhs=tok[:, g * 256:(g + 1) * 256], start=True, stop=True)
                A("act", "copy", [paccB[pq]], [qbsB[k]], out=qbs[:, k, :], in_=pacc[:, pq, 0:256])
                A("dve", "tensor_tensor", [kvtB[k], qbsB[k]], [prodB[k]], out=prod[:, k, :], in0=kvt[:, k, 0:256], in1=qbs[:, k, :], op=ALU.mult)
                A("dve", "tensor_reduce", [prodB[k]], [sc4B[k]], out=sc4[:, k, :], in_=prod[:, k, :].rearrange("p (h d) -> p h d", d=64), axis=AX.X, op=ALU.add)
                A("act", "activation", [sc4B[k]], [pb4B[k]], out=pb4[:, k, :], in_=sc4[:, k, :], func=AF.Exp, scale=0.125)

            def sp2(i):
                g, b = its[i]
                k = i % 4
                A("dve", "tensor_tensor", [kvtB[k], pb4B[k]], [pvtB[k]], out=pvt[:, k, :].rearrange("p (h d) -> p h d", d=64),
                  in0=kvt[:, k, 256:512].rearrange("p (h d) -> p h d", d=64), in1=pb4[:, k, :].unsqueeze(2).broadcast_to([128, 4, 64]), op=ALU.mult)
                A("pe", "matmul", [cB, pvtB[k]], [pSB[0]], out=pS[:16, 0, 0:256], lhsT=Eoh[:, b, :], rhs=pvt[:, k, :], start=(b == 0), stop=(b == 15))
                A("pe", "matmul", [cB, pb4B[k]], [pSB[1]], out=pS[:16, 1, 0:4], lhsT=Eoh[:, b, :], rhs=pb4[:, k, :], start=(b == 0), stop=(b == 15))
                if b == 15:
                    A("dve", "tensor_tensor", [tokB, snB], [osB], out=osum[:, g * 256:(g + 1) * 256].rearrange("b (h d) -> b h d", d=64),
                      in0=tok[:, 1536 + g * 256:1536 + (g + 1) * 256].rearrange("b (h d) -> b h d", d=64),
                      in1=pn12[:, g * 4:(g + 1) * 4].unsqueeze(2).broadcast_to([16, 4, 64]), op=ALU.mult)
                    A("dve", "tensor_tensor", [osB, pSB[0]], [osB], out=osum[:, g * 256:(g + 1) * 256], in0=pS[:16, 0, 0:256], in1=osum[:, g * 256:(g + 1) * 256], op=ALU.add)
                    A("dve", "tensor_tensor", [snB, pSB[1]], [osB], out=rsum[:, g * 4:(g + 1) * 4], in0=pS[:16, 1, 0:4], in1=pn12[:, g * 4:(g + 1) * 4], op=ALU.add)

            for i in range(len(its) + 1):
                if i < len(its):
                    sp1(i)
                if i >= 1:
                    sp2(i - 1)
            A("dve", "tensor_tensor", [osB], [osB], out=rtot[:], in0=rsum[:, 0:4], in1=rsum[:, 4:8], op=ALU.add)
            A("dve", "tensor_tensor", [osB], [osB], out=rtot[:], in0=rtot[:], in1=rsum[:, 8:12], op=ALU.add)
            A("dve", "reciprocal", [osB], [osB], out=rtot[:], in_=rtot[:])
            A("dve", "tensor_tensor", [osB], [atsB], out=attn_s[:].rearrange("b (g h d) -> b g h d", g=3, h=4),
              in0=osum[:].rearrange("b (g h d) -> b g h d", g=3, h=4), in1=rtot[:].unsqueeze(1).unsqueeze(3).broadcast_to([16, 3, 4, 64]), op=ALU.mult)
            pb = nxt("pt", 4)
            for c in range(6):
                A("pe", "transpose", [atsB, cB], [pTB[pb]], out=pT[:, pb, c * 16:(c + 1) * 16], in_=attn_s[:, c * 128:(c + 1) * 128], identity=identb[:16, :16])
            A("act", "copy", [pTB[pb]], [attnTsB], out=attnTs[:], in_=pT[:, pb, 0:96].rearrange("p (c t) -> p c t", t=16))
            pa = nxt("pacc", 4)
            A("pe", "matmul", [cB, tokB], [paccB[pa]], out=pacc[:16, pa, :], lhsT=identf[0:16, 0:16], rhs=tok[:, 2816:3328], start=True, stop=True)
            ks = layernorm_rows(pa, 16)
            DMA("sp", vrs, stg[:16, ks, :], [stgB[ks]], [])
            A("dve", "tensor_tensor", [stgB[ks], w0B], [gmsB], out=gms[:].rearrange("b (g d) -> b g d", g=4), in0=stg[:16, ks, :].rearrange("b (g d) -> b g d", g=4), in1=ws0[:].broadcast_to([16, 4, 128]), op=ALU.mult)
            A("dve", "tensor_tensor", [gmsB, w0B], [gmsB], out=gms[:].rearrange("b (g d) -> b g d", g=4), in0=gms[:].rearrange("b (g d) -> b g d", g=4), in1=bs0[:].broadcast_to([16, 4, 128]), op=ALU.add)
            A("dve", "tensor_tensor", [gmsB, tokB], [gmsB], out=gmsb[:], in0=gms[:], in1=tok[:, 2304:2816], op=ALU.mult)
            pb = nxt("pt", 4)
            for c in range(4):
                A("pe", "transpose", [gmsB, cB], [pTB[pb]], out=pT[:, pb, c * 16:(c + 1) * 16], in_=gmsb[:, c * 128:(c + 1) * 128], identity=identb[:16, :16])
            A("act", "copy", [pTB[pb]], [bigB[GOFF + i] for i in range(4)], out=big[:, GOFF:GOFF + 4, 0:16], in_=pT[:, pb, 0:64].rearrange("p (c t) -> p c t", t=16))
            merged_and_out(lambda kc: (attnTs[:, kc, :], attnTsB))
            ffn("f2", 2, 2)
            final(lambda s: ys)

    kT = [sb("kT%d" % g, [128, 2, NS[g], 512], BF16) for g in range(3)]
    kTB = [[[Buf("kT%d_%d_%d" % (g, p, s)) for s in range(NS[g])] for p in range(2)] for g in range(3)]
    V = [sb("V%d" % g, [128, NS[g], 4, 256], BF16) for g in range(3)]
    VB = [[[Buf("V%d_%d_%d" % (g, s, u)) for u in range(4)] for s in range(NS[g])] for g in range(3)]
    Ot = sb("Ot", [128, 6, 512], BF16); OtB = [Buf("Ot%d" % i) for i in range(6)]
    RS = sb("RS", [128, 2, 512], F32); RSB = [Buf("RS0"), Buf("RS1")]
    PT = sb("PT", [128, 3, 640], BF16); PTB = [Buf("PT0"), Buf("PT1"), Buf("PT2")]
    print("sbuf bytes remaining", nc.sbuf_bytes_remaining)
    new_bufs = [b for g in kTB for p in g for b in p] + [b for g in VB for s in g for b in s] + OtB + RSB + PTB
    bar = sb("bar", [128, 8], F32)
    A("dve", "memset", [], samp_bufs + new_bufs, bar[:], 0.0)

    def tsel(ap, g, sub):
        if g == 0:
            return ap[:, sub * 128:(sub + 1) * 128]
        return ap[:, sub:512:4]

    def shp(ap, g):
        return ap

    def attention(seq, n):
        units = []
        for g in range(3):
            for p in range(2):
                for sub in range(4):
                    if g == 0:
                        pcs = []
                        if sub > 0:
                            pcs.append((n % 2, sub - 1))
                        elif n > 0:
                            pcs.append(((n - 1) % 2, 3))
                        pcs.append((n % 2, sub))
                        mask = m01[:, 256 - 128 * len(pcs):256]
                    elif g == 1:
                        pcs = ([((n - 1) % 2, sub)] if n > 0 else []) + [(n % 2, sub)]
                        mask = m01[:, 256 - 128 * len(pcs):256]
                    else:
                        pcs = [(m % 5, sub) for m in range(max(0, n - 4), n + 1)]
                        mask = m2[:, 640 - 128 * len(pcs):640]
                    for hh in range(2):
                        units.append((g, p, sub, hh, pcs, mask))
        SB3 = [(pSf, [pSB[0], pSB[1]]), (pTf, [pTB[0], pTB[2]]), (pAf, [paccB[2], paccB[3]])]

        def ph1(i):
            g, p, sub, hh, pcs, mask = units[i]
            c = g * 2 + p
            npc = len(pcs)
            k = i % 3
            psv, psb = SB3[k]
            hp = slice(64 * hh, 64 * hh + 64)
            q_ap = tsel(big[hp, QOFF + c, :], g, sub)
            for j, (sl, su) in enumerate(pcs):
                A("pe", "matmul", [kTB[g][p][sl], bigB[QOFF + c]], [psb[(j * 128) // 512]], out=psv[:, j * 128:(j + 1) * 128],
                  lhsT=tsel(kT[g][hp, p, sl, :], g, su), rhs=q_ap, start=True, stop=True)
            nb = (npc * 128 + 511) // 512
            A("act", "activation", psb[0:nb], [PTB[k]], out=PT[:, k, 0:npc * 128], in_=psv[:, 0:npc * 128], func=AF.Exp, scale=0.125)
            A("pool", "tensor_tensor", [PTB[k], cB], [PTB[k]], out=PT[:, k, 0:npc * 128], in0=PT[:, k, 0:npc * 128], in1=mask, op=ALU.mult)

        def ph2(i):
            g, p, sub, hh, pcs, mask = units[i]
            c = g * 2 + p
            npc = len(pcs)
            k = i % 3
            po = i % 2
            hp = slice(64 * hh, 64 * hh + 64)
            for j, (sl, su) in enumerate(pcs):
                A("pe", "matmul", [VB[g][sl][su], PTB[k]], [paccB[po]], out=pacc[:, po, 0:128], lhsT=V[g][:, sl, su, p * 128:(p + 1) * 128],
                  rhs=PT[:, k, j * 128:(j + 1) * 128], start=(j == 0), stop=(j == npc - 1))
            for j in range(npc):
                A("pe", "matmul", [cB, PTB[k]], [paccB[po]], out=pacc[:, po, 128:256], lhsT=onesb[:], rhs=PT[:, k, j * 128:(j + 1) * 128],
                  start=(j == 0), stop=(j == npc - 1))
            A("act", "copy", [paccB[po]], [OtB[c]], out=tsel(Ot[hp, c, :], g, sub), in_=pacc[hp, po, 0:128])
            if g == 0:
                A("dve", "tensor_copy", [paccB[po]], [RSB[p]], out=tsel(RS[hp, p, :], g, sub), in_=pacc[hp, po, 128:256])
            else:
                A("dve", "tensor_tensor", [paccB[po], RSB[p]], [RSB[p]], out=tsel(RS[hp, p, :], g, sub), in0=pacc[hp, po, 128:256],
                  in1=tsel(RS[hp, p, :], g, sub), op=ALU.add)

        DEP = 2
        for i in range(len(units) + DEP):
            if i < len(units):
                ph1(i)
            if i - DEP >= 0:
                ph2(i - DEP)
        for p in range(2):
            A("dve", "reciprocal", [RSB[p]], [RSB[p]], out=RS[:, p, :], in_=RS[:, p, :])
        for c in range(6):
            A("pool", "tensor_tensor", [OtB[c], RSB[c % 2]], [OtB[c]], out=Ot[:, c, :], in0=Ot[:, c, :], in1=RS[:, c % 2, :], op=ALU.mult)

    def tokmix_p(seq, n):
        norm(1)
        for blk in range(3):
            wv, wB, _ = wblk(winv[:, :, blk * 512:(blk + 1) * 512], wbB["win"], (8, 512))
            for cc in range(4):
                c = blk * 4 + cc
                pa = nxt("pacc", 4)
                mm_acc(pa, 512, [(wv[:, kc, cc * 128:(cc + 1) * 128], hT[:, kc, :]) for kc in range(8)], [[wB, hTB[kc]] for kc in range(8)])
                if c < 6:
                    A("act", "copy", [paccB[pa]], [bigB[QOFF + c]], out=big[:, QOFF + c, :], in_=pacc[:, pa, :])
                else:
                    g, p = divmod(c - 6, 2)
                    A("act", "copy", [paccB[pa]], [kTB[g][p][n % NS[g]]], out=kT[g][:, p, n % NS[g], :], in_=pacc[:, pa, :])
        mark("tm_qk")
        kvv = wb["win"][:, 768:2304].rearrange("(kc p) (ab m) -> p kc ab m", p=128, ab=2)
        for g in range(3):
            wv, wB, flat = wblk(kvv[:, :, :, g * 256:(g + 1) * 256], wbB["win"], (8, 2, 256))
            w2 = flat.rearrange("p (kc m) -> p kc m", kc=8)
            for sub in range(4):
                pa = nxt("pacc", 4)
                want = (g == 0 and n == 7 and sub == 3) or (g == 1 and n == 7) or (g == 2 and n >= 4)
                sl = n % NS[g]
                if want:
                    mm_acc(pa, 512, [(tsel(hT[:, kc, :], g, sub), w2[:, kc, :]) for kc in range(8)], [[wB, hTB[kc]] for kc in range(8)])
                    A("act", "copy", [paccB[pa]], [VB[g][sl][sub]], out=V[g][:, sl, sub, :], in_=pacc[:, pa, 256:512])
                else:
                    mm_acc(pa, 256, [(tsel(hT[:, kc, :], g, sub), w2[:, kc, 256:512]) for kc in range(8)], [[wB, hTB[kc]] for kc in range(8)])
                    A("act", "copy", [paccB[pa]], [VB[g][sl][sub]], out=V[g][:, sl, sub, :], in_=pacc[:, pa, 0:256])
                if want:
                    ks = nxt("stg", 2)
                    A("dve", "tensor_copy", [paccB[pa]], [stgB[ks]], out=stg[:, ks, :], in_=pacc[:, pa, :])
                    if g == 0:
                        DMA("act", kvp[0][seq], stg[:, ks, :], [stgB[ks]], [])
                    elif g == 1:
                        DMA("act", kvp[1][seq, sub:512:4, :], stg[:, ks, :], [stgB[ks]], [])
                    else:
                        r0 = (n - 4) * 512 + sub
                        DMA("act", kvp[2][seq, r0:(n - 3) * 512:4, :], stg[:, ks, :], [stgB[ks]], [])
        mark("tm_kv")
        wv, wB, _ = wblk(winv[:, :, 2304:2816], wbB["win"], (8, 512))
        for cc in range(4):
            pa = nxt("pacc", 4)
            mm_acc(pa, 512, [(wv[:, kc, cc * 128:(cc + 1) * 128], hT[:, kc, :]) for kc in range(8)], [[wB, hTB[kc]] for kc in range(8)])
            A("act", "copy", [paccB[pa]], [bigB[UOFF + cc]], out=big[:, UOFF + cc, :], in_=pacc[:, pa, :])
        mark("tm_u")
        wv, wB, _ = wblk(winv[:, :, 2816:3328], wbB["win"], (8, 512))
        for s in range(4):
            pa = nxt("pacc", 4)
            mm_acc(pa, 512, [(hT[:, kc, s * 128:(s + 1) * 128], wv[:, kc, :]) for kc in range(8)], [[wB, hTB[kc]] for kc in range(8)])
            ks = layernorm_rows(pa, 128)
            if n == 7 and s == 3:
                DMA("act", vrp[seq], stg[:, ks, :], [stgB[ks]], [])
            kv_ = nxt("vnb", 2)
            A("act", "copy", [stgB[ks]], [vnbB[kv_]], out=vnb[:, kv_, :], in_=stg[:, ks, :])
            pa2 = nxt("pacc", 4)
            for g in range(4):
                A("pe", "matmul", [vnbB[kv_], wsTB], [paccB[pa2]], out=pacc[:, pa2, g * 128:(g + 1) * 128], lhsT=vnb[:, kv_, g * 128:(g + 1) * 128],
                  rhs=wsT[:, g, :], start=True, stop=True)
            k = nxt("tg", 2)
            A("dve", "tensor_tensor", [paccB[pa2], bsB], [tgB[k]], out=tg[:, k, :], in0=pacc[:, pa2, :], in1=bsbc[:].rearrange("p g i -> p (g i)"), op=ALU.add)
            A("pool", "tensor_tensor", [tgB[k]] + [bigB[UOFF + i] for i in range(4)], [bigB[GOFF + i] for i in range(4)],
              out=big[:, GOFF:GOFF + 4, s * 128:(s + 1) * 128], in0=tg[:, k, :].rearrange("p (g i) -> p g i", g=4),
              in1=big[:, UOFF:UOFF + 4, s * 128:(s + 1) * 128], op=ALU.mult)
        mark("tm_vb")
        if not _os.environ.get("SKIP_ATT"):
            attention(seq, n)
        mark("tm_att")
        merged_and_out(lambda kc: (Ot[:, kc, :], OtB[kc]))

    st.update(rows=128, nsub=4, TT=512, P=True, par=0)
    tiles = [(seq, n) for seq in range(n_seq) for n in range(n_tiles)]

    def load_x(t):
        seq, n = tiles[t]
        for s in range(4):
            i = 4 * (t % 2) + s
            DMA("sp", x[:, i, :], xp[seq, n * 512 + s * 128:n * 512 + (s + 1) * 128, :], [], [xB[i]])

    if tiles:
        load_x(0)
        st["par"] = 0
        prologue()
    for t, (seq, n) in enumerate(tiles):
        st["seq"] = seq
        st["n"] = n
        st["par"] = t % 2
        if n == 0:
            for gi in range(3):
                DMA("sp", gbc[:, gi, :], gsd[16 + seq, gi, :].partition_broadcast(128), [gsdB], [gbcB])
        for _ in range(3):
            if deferred:
                o_, i_ = deferred.pop(0)
                DMA("pool", o_, i_, [], [])
        ffn("f1", 0, 0, do_pro=False)
        if t + 1 < len(tiles):
            load_x(t + 1)
        tokmix_p(seq, n)
        mark("tm_wo")
        ffn("f2", 2, 2)
        mark("ffn2")
        if t + 1 < len(tiles):
            st["par"] = (t + 1) % 2
            prologue()
            st["par"] = t % 2
        final(lambda s: yp[seq, n * 512 + s * 128:n * 512 + (s + 1) * 128, :])
    while deferred:
        o_, i_ = deferred.pop(0)
        DMA("pool", o_, i_, [], [])
    S.emit(nc)
    return nc


_NC = {}


def _consts():
    bf = ml_dtypes.bfloat16
    j = np.arange(128)[:, None]; i = np.arange(128)[None, :]
    M0 = (j >= i); M1 = (j <= i)
    BD = (j % 4 == i % 4)
    BM0 = BD & ((j // 4) >= (i // 4)); BM1 = BD & ((j // 4) <= (i // 4))
    sel = np.zeros((18, 2, 128), np.float32); sel[16, 0, :] = 1; sel[17, 1, :] = 1
    E = np.zeros((128, 16, 16), np.float32)
    for b in range(16):
        E[:, b, b] = 1
    return {
        "c_identb": np.eye(128, dtype=np.float32).astype(bf), "c_identf": np.eye(128, dtype=np.float32),
        "c_m01": np.concatenate([M0, M1], 1).astype(np.float32).astype(bf),
        "c_m2": np.concatenate([BM0, BD, BD, BD, BM1], 1).astype(np.float32).astype(bf),
        "c_sel": sel, "c_E": E.astype(bf),
    }


def kernel(x_prompt, x_sample, c_prompt, c_sample, cache_kv_g0, cache_kv_g1, cache_kv_g2,
           ada_w, ada_b, norm_g, ffn1_up, ffn1_down, w_in, w_branch_a, w_branch_b, w_out,
           v_ln_g, v_ln_b, w_spatial, b_spatial, ffn2_up, ffn2_down, final_g):
    f = lambda a: np.ascontiguousarray(np.asarray(a, dtype=np.float32))
    if "nc" not in _NC:
        _NC["nc"] = build()
    nc = _NC["nc"]
    shared = {
        "ada_w": f(ada_w[0]), "ada_b": f(ada_b[0]).reshape(1, -1), "norm_g": f(norm_g[0]),
        "w_f1u": f(ffn1_up[0]), "w_f1d": f(ffn1_down[0]), "w_win": f(w_in[0]), "w_wba": f(w_branch_a[0]),
        "w_wbb": f(w_branch_b[0]), "w_wo": f(w_out[0]), "w_f2u": f(ffn2_up[0]), "w_f2d": f(ffn2_down[0]),
        "v_ln_g": f(v_ln_g[0]).reshape(1, -1), "v_ln_b": f(v_ln_b[0]).reshape(1, -1),
        "w_sp": f(w_spatial[0]), "b_sp": f(b_spatial[0]), "final_g": f(final_g).reshape(1, -1),
    }
    shared.update(_consts())
    x_prompt = np.asarray(x_prompt); x_sample = np.asarray(x_sample)
    in_maps = []
    for c in range(8):
        m = dict(shared)
        m["xp"] = f(x_prompt[2 * c:2 * c + 2])
        m["xs"] = f(x_sample[16 * c:16 * c + 16, 0])
        m["cv"] = f(np.concatenate([np.asarray(c_sample)[16 * c:16 * c + 16], np.asarray(c_prompt)[2 * c:2 * c + 2]], 0))
        m["ck0"] = f(np.asarray(cache_kv_g0)[0, 16 * c:16 * c + 16]).reshape(16, 128, 512)
        m["ck1"] = f(np.asarray(cache_kv_g1)[0, 16 * c:16 * c + 16]).reshape(16, 512, 512)
        m["ck2"] = f(np.asarray(cache_kv_g2)[0, 16 * c:16 * c + 16]).reshape(16, 2048, 512)
        in_maps.append(m)
    res = run_bass_kernel_spmd(nc, in_maps, core_ids=list(range(8)))
    R = res.results
    cat = lambda k: np.concatenate([np.asarray(r[k]) for r in R], 0)
    y_prompt = cat("yp")
    y_sample = cat("ys").reshape(128, 1, D)
    outs = [y_prompt, y_sample]
    for g, L in enumerate((128, 512, 2048)):
        outs.append(cat("kv%dp" % g).reshape(1, 16, L, 2, 4, 64))
    outs.append(cat("vrp").reshape(1, 16, 128, 512))
    for g, L in enumerate((128, 512, 2048)):
        outs.append(cat("kv%ds" % g).reshape(1, 128, L, 2, 4, 64))
    outs.append(cat("vrs").reshape(1, 128, 1, 512))
    return tuple(np.ascontiguousarray(o, dtype=np.float32) for o in outs)
```

```python
import numpy as np
import concourse.bass as bass
import concourse.mybir as mybir

F32 = mybir.dt.float32
BF16 = mybir.dt.bfloat16
AF = mybir.ActivationFunctionType
ALU = mybir.AluOpType
AX = mybir.AxisListType


class Buf:
    __slots__ = ("name", "w", "r", "excl")

    def __init__(self, name, excl=False):
        self.name = name
        self.excl = excl
        self.w = None
        self.r = {}


class Op:
    __slots__ = ("eng", "fn", "waits", "flag", "cnt", "dma", "dsem", "dval", "pos")


class Sched:
    ENGS = ("pe", "act", "dve", "pool", "sp")
    ND = 6

    def __init__(self):
        self.ops = {e: [] for e in self.ENGS}
        self.ndma = {e: 0 for e in self.ENGS}
        self.dma_ops = {e: [] for e in self.ENGS}

    def add(self, eng, fn, reads=(), writes=(), dma=False):
        op = Op()
        op.eng = eng
        op.fn = fn
        op.flag = False
        op.cnt = 0
        op.dma = dma
        op.dsem = None
        op.dval = 0
        op.pos = len(self.ops[eng])
        waits = []
        ex = [b for b in reads if b.excl]
        if ex:
            writes = list(writes) + [b for b in ex if b not in writes]
            reads = [b for b in reads if not b.excl]

        def need(p, kind):
            if p is op:
                return
            if p.dma:
                waits.append(p)
                return
            if p.eng == eng and not dma:
                if kind != "raw" or eng == "pe":
                    return
            p.flag = True
            waits.append(p)

        for b in reads:
            if b.w is not None:
                need(b.w, "raw")
        for b in writes:
            if b.w is not None:
                need(b.w, "waw")
            for k, r in b.r.items():
                if k == "dma":
                    for rr in r:
                        need(rr, "war")
                else:
                    need(r, "war")
        if dma:
            i = self.ndma[eng]
            op.dsem = i % self.ND
            op.dval = 16 * (i // self.ND + 1)
            if i >= self.ND:
                waits.append(self.dma_ops[eng][i - self.ND])
            self.ndma[eng] += 1
            self.dma_ops[eng].append(op)
        op.waits = waits
        for b in reads:
            if dma:
                b.r.setdefault("dma", []).append(op)
            else:
                b.r[eng] = op
        for b in writes:
            b.w = op
            b.r = {}
        self.ops[eng].append(op)
        return op

    def emit(self, nc):
        for e in self.ENGS:
            c = 0
            for op in self.ops[e]:
                if op.flag and not op.dma:
                    c += 1
                    op.cnt = c
        from contextlib import ExitStack
        with ExitStack() as es:
            csem = {e: es.enter_context(nc.semaphore("c_" + e)) for e in self.ENGS}
            dsem = {e: [es.enter_context(nc.semaphore("d_%s%d" % (e, i))) for i in range(self.ND)]
                    for e in self.ENGS if self.ndma[e] > 0}
            block = es.enter_context(nc.Block())
            sched = self

            def run(e, eng):
                seen = {}
                for op in sched.ops[e]:
                    req = {}
                    for p in op.waits:
                        if p.dma:
                            key = ("d", p.eng, p.dsem)
                            val = p.dval
                        else:
                            key = ("c", p.eng)
                            val = p.cnt
                        if req.get(key, 0) < val:
                            req[key] = val
                    for key, val in req.items():
                        if seen.get(key, 0) >= val:
                            continue
                        seen[key] = val
                        s = dsem[key[1]][key[2]] if key[0] == "d" else csem[key[1]]
                        eng.wait_ge(s, val)
                    ins = op.fn(eng)
                    if op.dma:
                        ins.then_inc(dsem[e][op.dsem], 16)
                    elif op.flag:
                        ins.then_inc(csem[e], 1)
                if sched.ndma[e] > 0:
                    last = {}
                    for op in sched.dma_ops[e]:
                        last[op.dsem] = op.dval
                    for k, v in last.items():
                        if seen.get(("d", e, k), 0) < v:
                            eng.wait_ge(dsem[e][k], v)

            @block.tensor
            def _(eng):
                run("pe", eng)

            @block.scalar
            def _(eng):
                run("act", eng)

            @block.vector
            def _(eng):
                run("dve", eng)

            @block.gpsimd
            def _(eng):
                run("pool", eng)

            @block.sync
            def _(eng):
                run("sp", eng)

import ml_dtypes
from contextlib import ExitStack
from concourse.bass_utils import run_bass_kernel_spmd

D = 1024
DFF = 2816
DIN = 5376
EPS = 1e-6
NSLOT = 4
NS = (2, 2, 5)
QOFF, UOFF, GOFF = 12, 18, 8


def build(n_seq=2, n_tiles=8, do_sample=True):
    nc = bass.Bass("TRN2", target_bir_lowering=False)
    S = Sched()

    def din(n, shp, dt=F32):
        return nc.dram_tensor(n, list(shp), dt, kind="ExternalInput").ap()

    def dout(n, shp):
        return nc.dram_tensor(n, list(shp), F32, kind="ExternalOutput").ap()

    def dint(n, shp, dt):
        return nc.dram_tensor(n, list(shp), dt, kind="Internal").ap()

    def sb(n, shp, dt):
        return nc.alloc_sbuf_tensor(n, list(shp), dt)

    import os as _os
    KSTOP = _os.environ.get("KSTOP", "")
    _stopflag = [False]

    def mark(name):
        if KSTOP and name == KSTOP:
            _stopflag[0] = True

    def _flat(l):
        o = []
        for b_ in l:
            if isinstance(b_, (list, tuple)):
                o.extend(_flat(b_))
            else:
                o.append(b_)
        return o

    def A(eng, meth, reads, writes, *args, **kw):
        if _stopflag[0]:
            return None
        return S.add(eng, (lambda e: getattr(e, meth)(*args, **kw)), _flat(reads), _flat(writes))

    def DMA(q, out, in_, reads, writes, **kw):
        if _stopflag[0]:
            return None
        return S.add(q, (lambda e: e.dma_start(out=out, in_=in_, **kw)), _flat(reads), _flat(writes), dma=True)

    xp = din("xp", [2, 4096, D]); xs = din("xs", [16, D]); cv = din("cv", [18, D])
    ck = [din("ck0", [16, 128, 512]), din("ck1", [16, 512, 512]), din("ck2", [16, 2048, 512])]
    ada_w = din("ada_w", [D, 9 * D]); ada_b = din("ada_b", [1, 9 * D]); norm_g = din("norm_g", [3, D])
    wshape = {"f1u": [D, 2 * DFF], "f1d": [DFF, D], "win": [D, DIN], "wba": [768, D], "wbb": [512, D],
              "wo": [D, D], "f2u": [D, 2 * DFF], "f2d": [DFF, D]}
    worder = ["f1u", "f1d", "win", "wba", "wbb", "wo", "f2u", "f2d"]
    wsrc = {k: din("w_" + k, wshape[k]) for k in worder}
    vlng = din("v_ln_g", [1, 512]); vlnb = din("v_ln_b", [1, 512])
    wsp = din("w_sp", [4, 128, 128]); bsp = din("b_sp", [4, 128]); fing = din("final_g", [1, D])
    c_identb = din("c_identb", [128, 128], BF16); c_identf = din("c_identf", [128, 128])
    c_m01 = din("c_m01", [128, 256], BF16); c_m2 = din("c_m2", [128, 640], BF16)
    c_sel = din("c_sel", [18, 2, 128]); c_E = din("c_E", [128, 16, 16], BF16)
    yp = dout("yp", [2, 4096, D]); ys = dout("ys", [16, D])
    kvp = [dout("kv0p", [2, 128, 512]), dout("kv1p", [2, 512, 512]), dout("kv2p", [2, 2048, 512])]
    vrp = dout("vrp", [2, 128, 512])
    kvs = [dout("kv0s", [16, 128, 512]), dout("kv1s", [16, 512, 512]), dout("kv2s", [16, 2048, 512])]
    vrs = dout("vrs", [16, 512])
    wb = {k: dint("b_" + k, wshape[k], BF16) for k in worder}
    wbB = {k: [] for k in worder}

    ring = sb("ring", [128, NSLOT, 5632], BF16); ringB = [[Buf("ring%d_%d" % (i, h)) for h in range(2)] for i in range(NSLOT)]
    x = sb("x", [128, 8, D], F32); xB = [Buf("x%d" % i) for i in range(8)]
    hT = sb("hT", [128, 8, 512], BF16); hTB = [Buf("hT%d" % i) for i in range(8)]
    big = sb("big", [128, 22, 512], BF16); bigB = [Buf("big%d" % i) for i in range(22)]
    ss = sb("ss", [128, 8], F32); ssB = [Buf("ss%d" % i) for i in range(8)]
    rs = sb("rs", [128, 8], F32); rsB = [Buf("rs%d" % i) for i in range(8)]
    ri = sb("ri", [128, 8], F32); riB = [Buf("ri%d" % i) for i in range(8)]
    sg = sb("sg", [128, 2, 512], BF16); sgB = [Buf("sg0"), Buf("sg1")]
    sga = sb("sga", [128, 2, 512], BF16); sgaB = [Buf("sga0"), Buf("sga1")]
    tg = sb("tg", [128, 2, 512], F32); tgB = [Buf("tg0"), Buf("tg1")]
    tA = sb("tA", [128, 2, 512], F32); tAB = [Buf("tA0"), Buf("tA1")]
    stg = sb("stg", [128, 2, 512], F32); stgB = [Buf("stg%d" % i) for i in range(2)]
    gsd = dint("gsd", [18, 3, D], F32); gsdB = Buf("gsd")
    gbc = sb("gbc", [128, 3, D], F32); gbcB = Buf("gbc")
    modT = sb("modT", [128, 6, 8, 18], F32); modB = Buf("modT")
    gT = sb("gT", [128, 3, 8], F32); gTB = Buf("gT")
    fg = sb("fg", [128, D], F32); fgB = Buf("fg")
    lng = sb("lng", [128, 512], F32); lnb = sb("lnb", [128, 512], F32); lnB = Buf("ln")
    bsbc = sb("bsbc", [128, 4, 128], F32); bsB = Buf("bs")
    wsT = sb("wsT", [128, 4, 128], BF16); wsTB = Buf("wsT")
    identb = sb("identb", [128, 128], BF16); identf = sb("identf", [128, 128], F32)
    m01 = sb("m01", [128, 256], BF16); m2 = sb("m2", [128, 640], BF16)
    sel = sb("sel", [18, 2, 128], F32); Eoh = sb("Eoh", [128, 16, 16], BF16)
    onesb = sb("onesb", [128, 128], BF16); onef = sb("onef", [1, 32], F32)
    cB = Buf("consts")
    bnst = sb("bnst", [128, 2, 6], F32); bnag = sb("bnag", [128, 2, 2], F32); lrs = sb("lrs", [128, 2, 2], F32)
    bnB = [Buf("bn0"), Buf("bn1")]; lrsB = [Buf("lrs0"), Buf("lrs1")]
    vnb = sb("vnb", [128, 2, 512], BF16); vnbB = [Buf("vnb0"), Buf("vnb1")]
    tmp16 = sb("tmp16", [128, 2, 16], F32); tmp16B = [Buf("t16a"), Buf("t16b")]

    pacc = nc.alloc_psum_tensor("pacc", [128, 4, 512], F32); paccB = [Buf("pacc%d" % i, True) for i in range(4)]
    pT = nc.alloc_psum_tensor("pT", [128, 4, 512], BF16); _pb0, _pb1 = Buf("pTb0", True), Buf("pTb1", True); pTB = [_pb0, _pb0, _pb1, _pb1]
    pS = nc.alloc_psum_tensor("pS", [128, 2, 512], F32); pSB = [Buf("pS0", True), Buf("pS1", True)]
    pTf = pT[:].bitcast(F32).rearrange("p a b -> p (a b)")
    pSf = pS[:].rearrange("p a b -> p (a b)")
    pAf = pacc[:, 2:4, :].rearrange("p a b -> p (a b)")
    cnt = {"pacc": 0, "ring": 0, "sg": 0, "sga": 0, "tg": 0, "tA": 0, "stg": 0, "bn": 0, "vnb": 0, "t16": 0, "pt": 0}

    def nxt(k, n):
        v = cnt[k] % n
        cnt[k] += 1
        if k == "pt":
            v = (0, 2, 1, 3)[v]
        return v

    def wblk(src, reads, dims, dt=BF16):
        s = nxt("ring", NSLOT)
        n = int(np.prod(dims))
        base = ring[:, s, :] if dt == BF16 else ring[:, s, :].bitcast(F32)
        flat = base[:, 0:n]
        if len(dims) == 2:
            v = flat.rearrange("p (a b) -> p a b", a=dims[0])
        else:
            v = flat.rearrange("p (a b c) -> p a b c", a=dims[0], b=dims[1])
        if len(dims) == 2:
            DMA("sp", v, src, reads, [ringB[s]])
        else:
            for ab in range(dims[1]):
                DMA("sp", v[:, :, ab, :], src[:, :, ab, :], reads, [ringB[s][ab]])
        return v, ringB[s], flat

    for dst, src in ((identb, c_identb), (identf, c_identf), (m01, c_m01), (m2, c_m2), (sel, c_sel), (Eoh, c_E)):
        DMA("sp", dst[:], src, [], [cB])
    A("dve", "memset", [], [cB], onesb[:], 1.0)
    A("dve", "memset", [], [cB], onef[:], 1.0)
    DMA("sp", fg[:], fing[0].partition_broadcast(128), [], [fgB])
    DMA("sp", lng[:], vlng[0].partition_broadcast(128), [], [lnB])
    DMA("sp", lnb[:], vlnb[0].partition_broadcast(128), [], [lnB])
    DMA("sp", bsbc[:], bsp.partition_broadcast(128), [], [bsB])
    for k in worder:
        R_, C_ = wshape[k]
        for r0 in range(0, R_, 128):
            for c0 in range(0, C_, 2048):
                c1 = min(C_, c0 + 2048)
                _b = Buf("wb_%s_%d_%d" % (k, r0, c0))
                wbB[k].append(_b)
                DMA("pool", wb[k][r0:r0 + 128, c0:c1], wsrc[k][r0:r0 + 128, c0:c1], [], [_b])

    esp = ExitStack()
    def sbp(n, shp, dt):
        return esp.enter_context(nc.sbuf_tensor(n, list(shp), dt))
    wsl = sbp("wsl", [128, 4, 128], F32); wslb = sbp("wslb", [128, 4, 128], BF16); wslB = Buf("wsl"); wslbB = Buf("wslb")
    DMA("sp", wsl[:], wsp.rearrange("g i j -> i g j"), [], [wslB])
    A("act", "copy", [wslB], [wslbB], out=wslb[:], in_=wsl[:])
    for g in range(4):
        A("pe", "transpose", [wslbB, cB], [pTB[0]], out=pT[:, 0, g * 128:(g + 1) * 128], in_=wslb[:, g, :], identity=identb[:])
    A("dve", "tensor_tensor", [pTB[0], cB], [wsTB], out=wsT[:], in0=pT[:, 0, :].rearrange("p (g i) -> p g i", g=4),
      in1=m01[:, 128:256].unsqueeze(1).broadcast_to([128, 4, 128]), op=ALU.mult)

    ng = sbp("ng", [3, D], F32); ngB = Buf("ng")
    DMA("sp", ng[:], norm_g, [], [ngB])
    pa = nxt("pacc", 4)
    for c in range(8):
        A("pe", "transpose", [ngB, cB], [paccB[pa]], out=pacc[:, pa, c * 3:(c + 1) * 3], in_=ng[:3, c * 128:(c + 1) * 128], identity=identf[:3, :3])
    A("dve", "tensor_copy", [paccB[pa]], [gTB], out=gT[:].rearrange("p w c -> p c w"), in_=pacc[:, pa, 0:24].rearrange("p (c w) -> p c w", w=3))

    cs = sbp("cs", [18, D], F32); cs2 = sbp("cs2", [18, D], F32); csB = Buf("cs"); cs2B = Buf("cs2")
    scT = sbp("scT", [128, 8, 18], F32); scTB = Buf("scT")
    bt = sbp("bt", [1, 2, 256], F32); btB = [Buf("bt0"), Buf("bt1")]
    mt = sbp("mt", [18, 2, 256], F32); mtB = [Buf("mt0"), Buf("mt1")]
    DMA("sp", cs[:], cv, [], [csB])
    A("act", "activation", [csB], [cs2B], out=cs2[:], in_=cs[:], func=AF.Silu)
    pa = nxt("pacc", 4)
    for kc in range(8):
        A("pe", "transpose", [cs2B, cB], [paccB[pa]], out=pacc[:, pa, kc * 18:(kc + 1) * 18], in_=cs2[:18, kc * 128:(kc + 1) * 128], identity=identf[:18, :18])
    A("dve", "tensor_copy", [paccB[pa]], [scTB], out=scT[:], in_=pacc[:, pa, 0:144].rearrange("p (k r) -> p k r", r=18))
    adav = ada_w.rearrange("(kc p) c -> p kc c", p=128)
    for t in range(36):
        i, off = divmod(t * 256, D)
        if (not do_sample) and False:
            pass
        wv, wB, _ = wblk(adav[:, :, t * 256:(t + 1) * 256], [], (8, 256), dt=F32)
        k = t % 2
        DMA("sp", bt[0:1, k, :], ada_b[0:1, t * 256:(t + 1) * 256], [], [btB[k]])
        pa = nxt("pacc", 4)
        for kc in range(8):
            A("pe", "matmul", [scTB, wB], [paccB[pa]], out=pacc[:18, pa, 0:256], lhsT=scT[:, kc, :], rhs=wv[:, kc, :], start=(kc == 0), stop=False)
        A("pe", "matmul", [cB, btB[k]], [paccB[pa]], out=pacc[:18, pa, 0:256], lhsT=onef[0:1, 0:18], rhs=bt[0:1, k, :], start=False, stop=True)
        if i in (2, 5, 8):
            A("act", "activation", [paccB[pa]], [mtB[k]], out=mt[:18, k, :], in_=pacc[:18, pa, 0:256], func=AF.Copy,
              scale=(1.0 if i == 5 else 0.5))
            DMA("sp", gsd[:, i // 3, off:off + 256], mt[:18, k, :], [mtB[k]], [gsdB])
        else:
            A("act", "copy", [paccB[pa]], [mtB[k]], out=mt[:18, k, :], in_=pacc[:18, pa, 0:256])
            which, kind = i // 3, i % 3
            pa2 = nxt("pacc", 4)
            for cc in range(2):
                A("pe", "transpose", [mtB[k], cB], [paccB[pa2]], out=pacc[:, pa2, cc * 18:(cc + 1) * 18], in_=mt[:18, k, cc * 128:(cc + 1) * 128], identity=identf[:18, :18])
            for cc in range(2):
                c = off // 128 + cc
                if kind == 1:
                    A("dve", "tensor_scalar", [paccB[pa2], gTB], [modB], out=modT[:, 2 * which + 1, c, :], in0=pacc[:, pa2, cc * 18:(cc + 1) * 18],
                      scalar1=1.0, scalar2=gT[:, which, c:c + 1], op0=ALU.add, op1=ALU.mult)
                else:
                    A("dve", "tensor_copy", [paccB[pa2]], [modB], out=modT[:, 2 * which, c, :], in_=pacc[:, pa2, cc * 18:(cc + 1) * 18])

    st = {"rows": 128, "nsub": 4, "TT": 512, "P": True, "seq": 0, "n": 0}

    def gate_ap(gi, c0, w):
        if st["P"]:
            return gbc[:, gi, c0:c0 + w], gbcB
        return st["gss"][0:16, gi, c0:c0 + w], gsdB2

    def mm_acc(pa, outw, pairs, reads, rows=128):
        n = len(pairs)
        for i, (l, r) in enumerate(pairs):
            A("pe", "matmul", reads[i], [paccB[pa]], out=pacc[:rows, pa, 0:outw], lhsT=l, rhs=r, start=(i == 0), stop=(i == n - 1))

    def XS(s):
        return 4 * st.get("par", 0) + s

    def xnv(s, rows):
        return big[:rows, 2 * s:2 * s + 2, :].rearrange("p a b -> p (a b)")

    def xnB_(s):
        return [bigB[2 * s], bigB[2 * s + 1]]

    def sq_of(s):
        rows = st["rows"]
        i = XS(s)
        if s % 2 == 0:
            A("act", "activation", [xB[i]], [sgB, ssB[i]], out=sg[:rows].rearrange("p a b -> p (a b)"), in_=x[:rows, i, :], func=AF.Square,
              accum_out=ss[:rows, i:i + 1])
        else:
            A("dve", "scalar_tensor_tensor", [xB[i]], [sgaB, ssB[i]], out=sga[:rows].rearrange("p a b -> p (a b)"), in0=x[:rows, i, :], scalar=1.0,
              in1=x[:rows, i, :], op0=ALU.mult, op1=ALU.mult, accum_out=ss[:rows, i:i + 1])

    def rs_of(s):
        rows = st["rows"]
        i = XS(s)
        A("act", "activation", [ssB[i]], [rsB[i]], out=rs[:rows, i:i + 1], in_=ss[:rows, i:i + 1], func=AF.Sqrt, scale=1.0 / D, bias=EPS)
        A("dve", "reciprocal", [rsB[i]], [riB[i]], out=ri[:rows, i:i + 1], in_=rs[:rows, i:i + 1])

    def prologue():
        rows, nsub = st["rows"], st["nsub"]
        for s in range(nsub):
            sq_of(s)
        for s in range(nsub):
            rs_of(s)
        for s in range(nsub):
            i = XS(s)
            if s % 2 == 0:
                A("act", "activation", [xB[i], riB[i]], [xnB_(s)], out=xnv(s, rows), in_=x[:rows, i, :], func=AF.Copy, scale=ri[:rows, i:i + 1])
            else:
                A("pool", "tensor_scalar", [xB[i], riB[i]], [xnB_(s)], out=xnv(s, rows), in0=x[:rows, i, :], scalar1=ri[:rows, i:i + 1],
                  scalar2=0.0, op0=ALU.mult, op1=ALU.add)

    def norm(which, do_pro=True):
        rows, nsub, TT = st["rows"], st["nsub"], st["TT"]
        if do_pro:
            prologue()
        for c in range(8):
            pb = nxt("pt", 4)
            for s in range(nsub):
                A("pe", "transpose", [xnB_(s), cB], [pTB[pb]], out=pT[:, pb, s * 128:s * 128 + rows], in_=xnv(s, rows)[:, c * 128:(c + 1) * 128],
                  identity=identb[:rows, :rows])
            if st["P"]:
                r0 = 16 + st["seq"]
                A("dve", "tensor_scalar", [pTB[pb], modB], [hTB[c]], out=hT[:, c, :], in0=pT[:, pb, :], scalar1=modT[:, 2 * which + 1, c, r0:r0 + 1],
                  scalar2=modT[:, 2 * which, c, r0:r0 + 1], op0=ALU.mult, op1=ALU.add)
            else:
                k = nxt("t16", 2)
                A("dve", "tensor_tensor", [pTB[pb], modB], [tmp16B[k]], out=tmp16[:, k, :], in0=pT[:, pb, 0:16], in1=modT[:, 2 * which + 1, c, 0:16], op=ALU.mult)
                A("dve", "tensor_tensor", [tmp16B[k], modB], [hTB[c]], out=hT[:, c, 0:16], in0=tmp16[:, k, :], in1=modT[:, 2 * which, c, 0:16], op=ALU.add)

    def resid(pa, s, c0, w, gi):
        rows = st["rows"]
        g, gB = gate_ap(gi, c0, w)
        k = nxt("tg", 2)
        A("dve", "tensor_tensor", [paccB[pa], gB], [tgB[k]], out=tg[:rows, k, 0:w], in0=pacc[:rows, pa, 0:w], in1=g[:rows], op=ALU.mult)
        i = XS(s)
        A("pool", "tensor_tensor", [tgB[k], xB[i]], [xB[i]], out=x[:rows, i, c0:c0 + w], in0=x[:rows, i, c0:c0 + w], in1=tg[:rows, k, 0:w], op=ALU.add)

    def ffn(wk, which, gi, do_pro=True):
        rows, nsub, TT = st["rows"], st["nsub"], st["TT"]
        norm(which, do_pro)
        mark("norm")
        upv = wb[wk + "u"].rearrange("(kc p) (ab n) -> p kc ab n", p=128, ab=2)
        for jb in range(11):
            wv, wB, _ = wblk(upv[:, :, :, jb * 256:(jb + 1) * 256], wbB[wk + "u"], (8, 2, 256))
            for jj in range(2):
                j = jb * 2 + jj
                pa = nxt("pacc", 4); pb = nxt("pacc", 4)
                mm_acc(pa, TT, [(wv[:, kc, 0, jj * 128:(jj + 1) * 128], hT[:, kc, 0:TT]) for kc in range(8)], [[wB, hTB[kc]] for kc in range(8)])
                mm_acc(pb, TT, [(wv[:, kc, 1, jj * 128:(jj + 1) * 128], hT[:, kc, 0:TT]) for kc in range(8)], [[wB, hTB[kc]] for kc in range(8)])
                k = nxt("sg", 2)
                A("act", "activation", [paccB[pa]], [sgB[k]], out=sg[:, k, 0:TT], in_=pacc[:, pa, 0:TT], func=AF.Silu)
                A("dve", "tensor_tensor", [sgB[k], paccB[pb]], [bigB[j]], out=big[:, j, 0:TT], in0=pacc[:, pb, 0:TT], in1=sg[:, k, 0:TT], op=ALU.mult)
        mark("up")
        dnv = wb[wk + "d"].rearrange("(j p) c -> p j c", p=128)
        for q in range(4):
            wv, wB, _ = wblk(dnv[:, :, q * 256:(q + 1) * 256], wbB[wk + "d"], (22, 256))
            for s in range(nsub):
                pa = nxt("pacc", 4)
                mm_acc(pa, 256, [(big[:, j, s * 128:s * 128 + rows], wv[:, j, :]) for j in range(22)], [[wB, bigB[j]] for j in range(22)], rows=rows)
                resid(pa, s, q * 256, 256, gi)
        mark("down")

    def final(dst_fn):
        rows, nsub = st["rows"], st["nsub"]
        for s in range(nsub):
            sq_of(s)
        for s in range(nsub):
            rs_of(s)
        for s in range(nsub):
            i = XS(s)
            A("dve", "scalar_tensor_tensor", [xB[i], riB[i], fgB], [xB[i]], out=x[:rows, i, :], in0=x[:rows, i, :], scalar=ri[:rows, i:i + 1],
              in1=fg[:rows, :], op0=ALU.mult, op1=ALU.mult)
            mark("fin_stt")
            DMA("act", dst_fn(s), x[:rows, i, :], [xB[i]], [])

    winv = wb["win"].rearrange("(kc p) c -> p kc c", p=128)

    def merged_and_out(attn_fn):
        rows, nsub, TT = st["rows"], st["nsub"], st["TT"]
        wbav = wb["wba"].rearrange("(kc p) c -> p kc c", p=128)
        wbbv = wb["wbb"].rearrange("(kc p) c -> p kc c", p=128)
        wov = wb["wo"].rearrange("(kc p) c -> p kc c", p=128)
        for cb in range(2):
            wga, wgaB, _ = wblk(winv[:, :, 3328 + cb * 512:3328 + (cb + 1) * 512], wbB["win"], (8, 512))
            wa, waB, _ = wblk(wbav[:, :, cb * 512:(cb + 1) * 512], wbB["wba"], (6, 512))
            wgb, wgbB, _ = wblk(winv[:, :, 4352 + cb * 512:4352 + (cb + 1) * 512], wbB["win"], (8, 512))
            wbv, wbvB, _ = wblk(wbbv[:, :, cb * 512:(cb + 1) * 512], wbB["wbb"], (4, 512))
            for cc in range(4):
                c = cb * 4 + cc
                cs_ = slice(cc * 128, (cc + 1) * 128)
                pg = nxt("pacc", 4)
                mm_acc(pg, TT, [(wga[:, kc, cs_], hT[:, kc, 0:TT]) for kc in range(8)], [[wgaB, hTB[kc]] for kc in range(8)])
                k1 = nxt("sga", 2)
                A("act", "activation", [paccB[pg]], [sgaB[k1]], out=sga[:, k1, 0:TT], in_=pacc[:, pg, 0:TT], func=AF.Sigmoid)
                pA = nxt("pacc", 4)
                prs = [attn_fn(kc) for kc in range(6)]
                mm_acc(pA, TT, [(wa[:, kc, cs_], prs[kc][0]) for kc in range(6)], [[waB, prs[kc][1]] for kc in range(6)])
                ka = nxt("tA", 2)
                A("dve", "tensor_tensor", [paccB[pA], sgaB[k1]], [tAB[ka]], out=tA[:, ka, 0:TT], in0=pacc[:, pA, 0:TT], in1=sga[:, k1, 0:TT], op=ALU.mult)
                pg2 = nxt("pacc", 4)
                mm_acc(pg2, TT, [(wgb[:, kc, cs_], hT[:, kc, 0:TT]) for kc in range(8)], [[wgbB, hTB[kc]] for kc in range(8)])
                k2 = nxt("sga", 2)
                A("act", "activation", [paccB[pg2]], [sgaB[k2]], out=sga[:, k2, 0:TT], in_=pacc[:, pg2, 0:TT], func=AF.Sigmoid)
                pBk = nxt("pacc", 4)
                mm_acc(pBk, TT, [(wbv[:, kc, cs_], big[:, GOFF + kc, 0:TT]) for kc in range(4)], [[wbvB, bigB[GOFF + kc]] for kc in range(4)])
                kb = nxt("tg", 2)
                A("dve", "tensor_tensor", [paccB[pBk], sgaB[k2]], [tgB[kb]], out=tg[:, kb, 0:TT], in0=pacc[:, pBk, 0:TT], in1=sga[:, k2, 0:TT], op=ALU.mult)
                A("pool", "tensor_tensor", [tAB[ka], tgB[kb]], [bigB[c]], out=big[:, c, 0:TT], in0=tA[:, ka, 0:TT], in1=tg[:, kb, 0:TT], op=ALU.add)
        mark("tm_merged")
        for hf in range(2):
            wv, wB, _ = wblk(wov[:, :, hf * 512:(hf + 1) * 512], wbB["wo"], (8, 512))
            for s in range(nsub):
                pa = nxt("pacc", 4)
                mm_acc(pa, 512, [(big[:, kc, s * 128:s * 128 + rows], wv[:, kc, :]) for kc in range(8)], [[wB, bigB[kc]] for kc in range(8)], rows=rows)
                resid(pa, s, hf * 512, 512, 1)

    def layernorm_rows(pa, rows):
        k = nxt("bn", 2)
        A("dve", "bn_stats", [paccB[pa]], [bnB[k]], out=bnst[:rows, k, :], in_=pacc[:rows, pa, :])
        A("dve", "bn_aggr", [bnB[k]], [bnB[k]], out=bnag[:rows, k, :], in_=bnst[:rows, k, :])
        A("act", "activation", [bnB[k]], [lrsB[k]], out=lrs[:rows, k, 0:1], in_=bnag[:rows, k, 1:2], func=AF.Sqrt, scale=1.0, bias=EPS)
        A("dve", "reciprocal", [lrsB[k]], [lrsB[k]], out=lrs[:rows, k, 1:2], in_=lrs[:rows, k, 0:1])
        ks = nxt("stg", 2)
        A("dve", "tensor_scalar", [paccB[pa], bnB[k], lrsB[k]], [stgB[ks]], out=stg[:rows, ks, :], in0=pacc[:rows, pa, :], scalar1=bnag[:rows, k, 0:1],
          scalar2=lrs[:rows, k, 1:2], op0=ALU.subtract, op1=ALU.mult)
        A("pool", "tensor_tensor", [stgB[ks], lnB], [stgB[ks]], out=stg[:rows, ks, :], in0=stg[:rows, ks, :], in1=lng[:rows, :], op=ALU.mult)
        A("pool", "tensor_tensor", [stgB[ks], lnB], [stgB[ks]], out=stg[:rows, ks, :], in0=stg[:rows, ks, :], in1=lnb[:rows, :], op=ALU.add)
        return ks

    samp_bufs = []
    deferred = []
    prep_bufs = [wslB, wslbB, ngB, csB, cs2B, scTB] + btB + mtB
    esp.close()
    gsdB2 = Buf("gss")
    with ExitStack() as es:
        def sbt(n, shp, dt):
            return es.enter_context(nc.sbuf_tensor(n, list(shp), dt))
        tok = sbt("tok", [16, 3328], F32); tokB = Buf("tok")
        kvt = sbt("kvt", [128, 4, 512], F32); kvtB = [Buf("kvt%d" % i) for i in range(4)]
        qbs = sbt("qbs", [128, 4, 256], F32); qbsB = [Buf("qbs%d" % i) for i in range(4)]
        prod = qbs; prodB = qbsB
        sc4 = sbt("sc4", [128, 4, 4], F32); sc4B = [Buf("sc4_%d" % i) for i in range(4)]
        pb4 = sbt("pb4", [128, 4, 4], BF16); pb4B = [Buf("pb4_%d" % i) for i in range(4)]
        pvt = sbt("pvt", [128, 4, 256], BF16); pvtB = [Buf("pvt%d" % i) for i in range(4)]
        sn12 = sbt("sn12", [16, 12], F32); pn12 = sbt("pn12", [16, 12], F32); snB = Buf("sn")
        osum = sbt("osum", [16, 768], F32); qk12 = osum; rsum = sbt("rsum", [16, 12], F32); osB = Buf("osum")
        rtot = sbt("rtot", [16, 4], F32); attn_s = sbt("attn_s", [16, 768], BF16); atsB = Buf("attn_s")
        attnTs = sbt("attnTs", [128, 6, 16], BF16); attnTsB = Buf("attnTs")
        gms = sbt("gms", [16, 512], F32); gmsb = sbt("gmsb", [16, 512], BF16); gmsB = Buf("gms")
        ws0 = sbt("ws0", [16, 4, 1], F32); bs0 = sbt("bs0", [16, 4, 1], F32); w0B = Buf("w0")
        gss = sbt("gss", [16, 3, D], F32); st["gss"] = gss
        bar0 = sbt("bar0", [128, 8], F32)
        samp_bufs = [gsdB2, tokB, snB, osB, atsB, attnTsB, gmsB, w0B] + kvtB + qbsB + prodB + sc4B + pb4B + pvtB

        A("dve", "memset", [], prep_bufs + samp_bufs, bar0[:], 0.0)
        if do_sample:
            st.update(rows=16, nsub=1, TT=16, P=False)
            DMA("sp", gss[:], gsd[0:16], [gsdB], [gsdB2])
            DMA("sp", x[:16, 0, :], xs, [], [xB[0]])
            DMA("sp", ws0[:], wsp[:, 0, 0:1].partition_broadcast(16), [], [w0B], allow_slow_non_contiguous=True)
            DMA("sp", bs0[:], bsp[:, 0:1].partition_broadcast(16), [], [w0B], allow_slow_non_contiguous=True)
            for g, L in enumerate((128, 512, 2048)):
                for b0 in range(16):
                    deferred.append((kvs[g][b0:b0 + 1, 0:L - 1, :], ck[g][b0:b0 + 1, 1:L, :]))
            ffn("f1", 0, 0)
            norm(1)
            for blk in range(7):
                c0 = blk * 512; w = min(512, 3328 - c0)
                wv, wB, _ = wblk(winv[:, :, c0:c0 + w], wbB["win"], (8, w))
                pa = nxt("pacc", 4)
                mm_acc(pa, w, [(hT[:, kc, 0:16], wv[:, kc, :]) for kc in range(8)], [[wB, hTB[kc]] for kc in range(8)], rows=16)
                A("act", "copy", [paccB[pa]], [tokB], out=tok[:, c0:c0 + w], in_=pacc[:16, pa, 0:w])
            for g, L in enumerate((128, 512, 2048)):
                DMA("sp", kvs[g][:, L - 1, 0:256], tok[:, 768 + g * 256:768 + (g + 1) * 256], [tokB], [])
                DMA("sp", kvs[g][:, L - 1, 256:512], tok[:, 1536 + g * 256:1536 + (g + 1) * 256], [tokB], [])
            A("dve", "tensor_tensor", [tokB], [snB], out=qk12[:], in0=tok[:, 0:768], in1=tok[:, 768:1536], op=ALU.mult)
            A("dve", "tensor_reduce", [snB], [snB], out=sn12[:], in_=qk12[:].rearrange("b (h d) -> b h d", d=64), axis=AX.X, op=ALU.add)
            A("act", "activation", [snB], [snB], out=pn12[:], in_=sn12[:], func=AF.Exp, scale=0.125)
            its = [(g, b) for g in range(3) for b in range(16)]

            def sp1(i):
                g, b = its[i]
                k = i % 4
                L = (128, 512, 2048)[g]
                dil = (1, 4, 16)[g]
                DMA("sp", kvt[:, k, :], ck[g][b, 0:L:dil, :], [], [kvtB[k]])
                pq = nxt("pacc", 4)
                A("pe", "matmul", [cB, tokB], [paccB[pq]], out=pacc[:, pq, 0:256], lhsT=identf[0:16, b:b + 1].broadcast_to([16, 128]),
                  rhs=tok[:, g * 256:(g + 1) * 256], start=True, stop=True)
                A("act", "copy", [paccB[pq]], [qbsB[k]], out=qbs[:, k, :], in_=pacc[:, pq, 0:256])
                A("dve", "tensor_tensor", [kvtB[k], qbsB[k]], [prodB[k]], out=prod[:, k, :], in0=kvt[:, k, 0:256], in1=qbs[:, k, :], op=ALU.mult)
                A("dve", "tensor_reduce", [prodB[k]], [sc4B[k]], out=sc4[:, k, :], in_=prod[:, k, :].rearrange("p (h d) -> p h d", d=64), axis=AX.X, op=ALU.add)
                A("act", "activation", [sc4B[k]], [pb4B[k]], out=pb4[:, k, :], in_=sc4[:, k, :], func=AF.Exp, scale=0.125)

            def sp2(i):
                g, b = its[i]
                k = i % 4
                A("dve", "tensor_tensor", [kvtB[k], pb4B[k]], [pvtB[k]], out=pvt[:, k, :].rearrange("p (h d) -> p h d", d=64),
                  in0=kvt[:, k, 256:512].rearrange("p (h d) -> p h d", d=64), in1=pb4[:, k, :].unsqueeze(2).broadcast_to([128, 4, 64]), op=ALU.mult)
                A("pe", "matmul", [cB, pvtB[k]], [pSB[0]], out=pS[:16, 0, 0:256], lhsT=Eoh[:, b, :], rhs=pvt[:, k, :], start=(b == 0), stop=(b == 15))
                A("pe", "matmul", [cB, pb4B[k]], [pSB[1]], out=pS[:16, 1, 0:4], lhsT=Eoh[:, b, :], rhs=pb4[:, k, :], start=(b == 0), stop=(b == 15))
                if b == 15:
                    A("dve", "tensor_tensor", [tokB, snB], [osB], out=osum[:, g * 256:(g + 1) * 256].rearrange("b (h d) -> b h d", d=64),
                      in0=tok[:, 1536 + g * 256:1536 + (g + 1) * 256].rearrange("b (h d) -> b h d", d=64),
                      in1=pn12[:, g * 4:(g + 1) * 4].unsqueeze(2).broadcast_to([16, 4, 64]), op=ALU.mult)
                    A("dve", "tensor_tensor", [osB, pSB[0]], [osB], out=osum[:, g * 256:(g + 1) * 256], in0=pS[:16, 0, 0:256], in1=osum[:, g * 256:(g + 1) * 256], op=ALU.add)
                    A("dve", "tensor_tensor", [snB, pSB[1]], [osB], out=rsum[:, g * 4:(g + 1) * 4], in0=pS[:16, 1, 0:4], in1=pn12[:, g * 4:(g + 1) * 4], op=ALU.add)

            for i in range(len(its) + 1):
                if i < len(its):
                    sp1(i)
                if i >= 1:
                    sp2(i - 1)
            A("dve", "tensor_tensor", [osB], [osB], out=rtot[:], in0=rsum[:, 0:4], in1=rsum[:, 4:8], op=ALU.add)
            A("dve", "tensor_tensor", [osB], [osB], out=rtot[:], in0=rtot[:], in1=rsum[:, 8:12], op=ALU.add)
            A("dve", "reciprocal", [osB], [osB], out=rtot[:], in_=rtot[:])
            A("dve", "tensor_tensor", [osB], [atsB], out=attn_s[:].rearrange("b (g h d) -> b g h d", g=3, h=4),
              in0=osum[:].rearrange("b (g h d) -> b g h d", g=3, h=4), in1=rtot[:].unsqueeze(1).unsqueeze(3).broadcast_to([16, 3, 4, 64]), op=ALU.mult)
            pb = nxt("pt", 4)
            for c in range(6):
                A("pe", "transpose", [atsB, cB], [pTB[pb]], out=pT[:, pb, c * 16:(c + 1) * 16], in_=attn_s[:, c * 128:(c + 1) * 128], identity=identb[:16, :16])
            A("act", "copy", [pTB[pb]], [attnTsB], out=attnTs[:], in_=pT[:, pb, 0:96].rearrange("p (c t) -> p c t", t=16))
            pa = nxt("pacc", 4)
            A("pe", "matmul", [cB, tokB], [paccB[pa]], out=pacc[:16, pa, :], lhsT=identf[0:16, 0:16], rhs=tok[:, 2816:3328], start=True, stop=True)
            ks = layernorm_rows(pa, 16)
            DMA("sp", vrs, stg[:16, ks, :], [stgB[ks]], [])
            A("dve", "tensor_tensor", [stgB[ks], w0B], [gmsB], out=gms[:].rearrange("b (g d) -> b g d", g=4), in0=stg[:16, ks, :].rearrange("b (g d) -> b g d", g=4), in1=ws0[:].broadcast_to([16, 4, 128]), op=ALU.mult)
            A("dve", "tensor_tensor", [gmsB, w0B], [gmsB], out=gms[:].rearrange("b (g d) -> b g d", g=4), in0=gms[:].rearrange("b (g d) -> b g d", g=4), in1=bs0[:].broadcast_to([16, 4, 128]), op=ALU.add)
            A("dve", "tensor_tensor", [gmsB, tokB], [gmsB], out=gmsb[:], in0=gms[:], in1=tok[:, 2304:2816], op=ALU.mult)
            pb = nxt("pt", 4)
            for c in range(4):
                A("pe", "transpose", [gmsB, cB], [pTB[pb]], out=pT[:, pb, c * 16:(c + 1) * 16], in_=gmsb[:, c * 128:(c + 1) * 128], identity=identb[:16, :16])
            A("act", "copy", [pTB[pb]], [bigB[GOFF + i] for i in range(4)], out=big[:, GOFF:GOFF + 4, 0:16], in_=pT[:, pb, 0:64].rearrange("p (c t) -> p c t", t=16))
            merged_and_out(lambda kc: (attnTs[:, kc, :], attnTsB))
            ffn("f2", 2, 2)
            final(lambda s: ys)

    kT = [sb("kT%d" % g, [128, 2, NS[g], 512], BF16) for g in range(3)]
    kTB = [[[Buf("kT%d_%d_%d" % (g, p, s)) for s in range(NS[g])] for p in range(2)] for g in range(3)]
    V = [sb("V%d" % g, [128, NS[g], 4, 256], BF16) for g in range(3)]
    VB = [[[Buf("V%d_%d_%d" % (g, s, u)) for u in range(4)] for s in range(NS[g])] for g in range(3)]
    Ot = sb("Ot", [128, 6, 512], BF16); OtB = [Buf("Ot%d" % i) for i in range(6)]
    RS = sb("RS", [128, 2, 512], F32); RSB = [Buf("RS0"), Buf("RS1")]
    PT = sb("PT", [128, 3, 640], BF16); PTB = [Buf("PT0"), Buf("PT1"), Buf("PT2")]
    print("sbuf bytes remaining", nc.sbuf_bytes_remaining)
    new_bufs = [b for g in kTB for p in g for b in p] + [b for g in VB for s in g for b in s] + OtB + RSB + PTB
    bar = sb("bar", [128, 8], F32)
    A("dve", "memset", [], samp_bufs + new_bufs, bar[:], 0.0)

    def tsel(ap, g, sub):
        if g == 0:
            return ap[:, sub * 128:(sub + 1) * 128]
        return ap[:, sub:512:4]

    def shp(ap, g):
        return ap

    def attention(seq, n):
        units = []
        for g in range(3):
            for p in range(2):
                for sub in range(4):
                    if g == 0:
                        pcs = []
                        if sub > 0:
                            pcs.append((n % 2, sub - 1))
                        elif n > 0:
                            pcs.append(((n - 1) % 2, 3))
                        pcs.append((n % 2, sub))
                        mask = m01[:, 256 - 128 * len(pcs):256]
                    elif g == 1:
                        pcs = ([((n - 1) % 2, sub)] if n > 0 else []) + [(n % 2, sub)]
                        mask = m01[:, 256 - 128 * len(pcs):256]
                    else:
                        pcs = [(m % 5, sub) for m in range(max(0, n - 4), n + 1)]
                        mask = m2[:, 640 - 128 * len(pcs):640]
                    for hh in range(2):
                        units.append((g, p, sub, hh, pcs, mask))
        SB3 = [(pSf, [pSB[0], pSB[1]]), (pTf, [pTB[0], pTB[2]]), (pAf, [paccB[2], paccB[3]])]

        def ph1(i):
            g, p, sub, hh, pcs, mask = units[i]
            c = g * 2 + p
            npc = len(pcs)
            k = i % 3
            psv, psb = SB3[k]
            hp = slice(64 * hh, 64 * hh + 64)
            q_ap = tsel(big[hp, QOFF + c, :], g, sub)
            for j, (sl, su) in enumerate(pcs):
                A("pe", "matmul", [kTB[g][p][sl], bigB[QOFF + c]], [psb[(j * 128) // 512]], out=psv[:, j * 128:(j + 1) * 128],
                  lhsT=tsel(kT[g][hp, p, sl, :], g, su), rhs=q_ap, start=True, stop=True)
            nb = (npc * 128 + 511) // 512
            A("act", "activation", psb[0:nb], [PTB[k]], out=PT[:, k, 0:npc * 128], in_=psv[:, 0:npc * 128], func=AF.Exp, scale=0.125)
            A("pool", "tensor_tensor", [PTB[k], cB], [PTB[k]], out=PT[:, k, 0:npc * 128], in0=PT[:, k, 0:npc * 128], in1=mask, op=ALU.mult)

        def ph2(i):
            g, p, sub, hh, pcs, mask = units[i]
            c = g * 2 + p
            npc = len(pcs)
            k = i % 3
            po = i % 2
            hp = slice(64 * hh, 64 * hh + 64)
            for j, (sl, su) in enumerate(pcs):
                A("pe", "matmul", [VB[g][sl][su], PTB[k]], [paccB[po]], out=pacc[:, po, 0:128], lhsT=V[g][:, sl, su, p * 128:(p + 1) * 128],
                  rhs=PT[:, k, j * 128:(j + 1) * 128], start=(j == 0), stop=(j == npc - 1))
            for j in range(npc):
                A("pe", "matmul", [cB, PTB[k]], [paccB[po]], out=pacc[:, po, 128:256], lhsT=onesb[:], rhs=PT[:, k, j * 128:(j + 1) * 128],
                  start=(j == 0), stop=(j == npc - 1))
            A("act", "copy", [paccB[po]], [OtB[c]], out=tsel(Ot[hp, c, :], g, sub), in_=pacc[hp, po, 0:128])
            if g == 0:
                A("dve", "tensor_copy", [paccB[po]], [RSB[p]], out=tsel(RS[hp, p, :], g, sub), in_=pacc[hp, po, 128:256])
            else:
                A("dve", "tensor_tensor", [paccB[po], RSB[p]], [RSB[p]], out=tsel(RS[hp, p, :], g, sub), in0=pacc[hp, po, 128:256],
                  in1=tsel(RS[hp, p, :], g, sub), op=ALU.add)

        DEP = 2
        for i in range(len(units) + DEP):
            if i < len(units):
                ph1(i)
            if i - DEP >= 0:
                ph2(i - DEP)
        for p in range(2):
            A("dve", "reciprocal", [RSB[p]], [RSB[p]], out=RS[:, p, :], in_=RS[:, p, :])
        for c in range(6):
            A("pool", "tensor_tensor", [OtB[c], RSB[c % 2]], [OtB[c]], out=Ot[:, c, :], in0=Ot[:, c, :], in1=RS[:, c % 2, :], op=ALU.mult)

    def tokmix_p(seq, n):
        norm(1)
        for blk in range(3):
            wv, wB, _ = wblk(winv[:, :, blk * 512:(blk + 1) * 512], wbB["win"], (8, 512))
            for cc in range(4):
                c = blk * 4 + cc
                pa = nxt("pacc", 4)
                mm_acc(pa, 512, [(wv[:, kc, cc * 128:(cc + 1) * 128], hT[:, kc, :]) for kc in range(8)], [[wB, hTB[kc]] for kc in range(8)])
                if c < 6:
                    A("act", "copy", [paccB[pa]], [bigB[QOFF + c]], out=big[:, QOFF + c, :], in_=pacc[:, pa, :])
                else:
                    g, p = divmod(c - 6, 2)
                    A("act", "copy", [paccB[pa]], [kTB[g][p][n % NS[g]]], out=kT[g][:, p, n % NS[g], :], in_=pacc[:, pa, :])
        mark("tm_qk")
        kvv = wb["win"][:, 768:2304].rearrange("(kc p) (ab m) -> p kc ab m", p=128, ab=2)
        for g in range(3):
            wv, wB, flat = wblk(kvv[:, :, :, g * 256:(g + 1) * 256], wbB["win"], (8, 2, 256))
            w2 = flat.rearrange("p (kc m) -> p kc m", kc=8)
            for sub in range(4):
                pa = nxt("pacc", 4)
                want = (g == 0 and n == 7 and sub == 3) or (g == 1 and n == 7) or (g == 2 and n >= 4)
                sl = n % NS[g]
                if want:
                    mm_acc(pa, 512, [(tsel(hT[:, kc, :], g, sub), w2[:, kc, :]) for kc in range(8)], [[wB, hTB[kc]] for kc in range(8)])
                    A("act", "copy", [paccB[pa]], [VB[g][sl][sub]], out=V[g][:, sl, sub, :], in_=pacc[:, pa, 256:512])
                else:
                    mm_acc(pa, 256, [(tsel(hT[:, kc, :], g, sub), w2[:, kc, 256:512]) for kc in range(8)], [[wB, hTB[kc]] for kc in range(8)])
                    A("act", "copy", [paccB[pa]], [VB[g][sl][sub]], out=V[g][:, sl, sub, :], in_=pacc[:, pa, 0:256])
                if want:
                    ks = nxt("stg", 2)
                    A("dve", "tensor_copy", [paccB[pa]], [stgB[ks]], out=stg[:, ks, :], in_=pacc[:, pa, :])
                    if g == 0:
                        DMA("act", kvp[0][seq], stg[:, ks, :], [stgB[ks]], [])
                    elif g == 1:
                        DMA("act", kvp[1][seq, sub:512:4, :], stg[:, ks, :], [stgB[ks]], [])
                    else:
                        r0 = (n - 4) * 512 + sub
                        DMA("act", kvp[2][seq, r0:(n - 3) * 512:4, :], stg[:, ks, :], [stgB[ks]], [])
        mark("tm_kv")
        wv, wB, _ = wblk(winv[:, :, 2304:2816], wbB["win"], (8, 512))
        for cc in range(4):
            pa = nxt("pacc", 4)
            mm_acc(pa, 512, [(wv[:, kc, cc * 128:(cc + 1) * 128], hT[:, kc, :]) for kc in range(8)], [[wB, hTB[kc]] for kc in range(8)])
            A("act", "copy", [paccB[pa]], [bigB[UOFF + cc]], out=big[:, UOFF + cc, :], in_=pacc[:, pa, :])
        mark("tm_u")
        wv, wB, _ = wblk(winv[:, :, 2816:3328], wbB["win"], (8, 512))
        pas = []
        for s in range(4):
            pa = nxt("pacc", 4)
            pas.append(pa)
            mm_acc(pa, 512, [(hT[:, kc, s * 128:(s + 1) * 128], wv[:, kc, :]) for kc in range(8)], [[wB, hTB[kc]] for kc in range(8)])
        kvs_ = {}

        def ln_part(s):
            ks = layernorm_rows(pas[s], 128)
            if n == 7 and s == 3:
                DMA("act", vrp[seq], stg[:, ks, :], [stgB[ks]], [])
            kv_ = nxt("vnb", 2)
            A("act", "copy", [stgB[ks]], [vnbB[kv_]], out=vnb[:, kv_, :], in_=stg[:, ks, :])
            kvs_[s] = kv_

        def sp_part(s):
            kv_ = kvs_[s]
            p2 = s % 2
            for g in range(4):
                A("pe", "matmul", [vnbB[kv_], wsTB], [pSB[p2]], out=pS[:, p2, g * 128:(g + 1) * 128], lhsT=vnb[:, kv_, g * 128:(g + 1) * 128],
                  rhs=wsT[:, g, :], start=True, stop=True)
            k = nxt("tg", 2)
            A("dve", "tensor_tensor", [pSB[p2], bsB], [tgB[k]], out=tg[:, k, :], in0=pS[:, p2, :], in1=bsbc[:].rearrange("p g i -> p (g i)"), op=ALU.add)
            A("pool", "tensor_tensor", [tgB[k]] + [bigB[UOFF + i] for i in range(4)], [bigB[GOFF + i] for i in range(4)],
              out=big[:, GOFF:GOFF + 4, s * 128:(s + 1) * 128], in0=tg[:, k, :].rearrange("p (g i) -> p g i", g=4),
              in1=big[:, UOFF:UOFF + 4, s * 128:(s + 1) * 128], op=ALU.mult)

        ln_part(0)
        ln_part(1)
        sp_part(0)
        ln_part(2)
        sp_part(1)
        ln_part(3)
        sp_part(2)
        sp_part(3)
        mark("tm_vb")
        if not _os.environ.get("SKIP_ATT"):
            attention(seq, n)
        mark("tm_att")
        merged_and_out(lambda kc: (Ot[:, kc, :], OtB[kc]))

    st.update(rows=128, nsub=4, TT=512, P=True, par=0)
    tiles = [(seq, n) for seq in range(n_seq) for n in range(n_tiles)]

    def load_x(t):
        seq, n = tiles[t]
        for s in range(4):
            i = 4 * (t % 2) + s
            DMA("sp", x[:, i, :], xp[seq, n * 512 + s * 128:n * 512 + (s + 1) * 128, :], [], [xB[i]])

    if tiles:
        load_x(0)
        st["par"] = 0
        prologue()
    for t, (seq, n) in enumerate(tiles):
        st["seq"] = seq
        st["n"] = n
        st["par"] = t % 2
        if n == 0:
            for gi in range(3):
                DMA("sp", gbc[:, gi, :], gsd[16 + seq, gi, :].partition_broadcast(128), [gsdB], [gbcB])
        for _ in range(3):
            if deferred:
                o_, i_ = deferred.pop(0)
                DMA("pool", o_, i_, [], [])
        ffn("f1", 0, 0, do_pro=False)
        if t + 1 < len(tiles):
            load_x(t + 1)
        tokmix_p(seq, n)
        mark("tm_wo")
        ffn("f2", 2, 2)
        mark("ffn2")
        if t + 1 < len(tiles):
            st["par"] = (t + 1) % 2
            prologue()
            st["par"] = t % 2
        final(lambda s: yp[seq, n * 512 + s * 128:n * 512 + (s + 1) * 128, :])
    while deferred:
        o_, i_ = deferred.pop(0)
        DMA("pool", o_, i_, [], [])
    S.emit(nc)
    return nc


_NC = {}


def _consts():
    bf = ml_dtypes.bfloat16
    j = np.arange(128)[:, None]; i = np.arange(128)[None, :]
    M0 = (j >= i); M1 = (j <= i)
    BD = (j % 4 == i % 4)
    BM0 = BD & ((j // 4) >= (i // 4)); BM1 = BD & ((j // 4) <= (i // 4))
    sel = np.zeros((18, 2, 128), np.float32); sel[16, 0, :] = 1; sel[17, 1, :] = 1
    E = np.zeros((128, 16, 16), np.float32)
    for b in range(16):
        E[:, b, b] = 1
    return {
        "c_identb": np.eye(128, dtype=np.float32).astype(bf), "c_identf": np.eye(128, dtype=np.float32),
        "c_m01": np.concatenate([M0, M1], 1).astype(np.float32).astype(bf),
        "c_m2": np.concatenate([BM0, BD, BD, BD, BM1], 1).astype(np.float32).astype(bf),
        "c_sel": sel, "c_E": E.astype(bf),
    }


def kernel(x_prompt, x_sample, c_prompt, c_sample, cache_kv_g0, cache_kv_g1, cache_kv_g2,
           ada_w, ada_b, norm_g, ffn1_up, ffn1_down, w_in, w_branch_a, w_branch_b, w_out,
           v_ln_g, v_ln_b, w_spatial, b_spatial, ffn2_up, ffn2_down, final_g):
    f = lambda a: np.ascontiguousarray(np.asarray(a, dtype=np.float32))
    if "nc" not in _NC:
        _NC["nc"] = build()
    nc = _NC["nc"]
    shared = {
        "ada_w": f(ada_w[0]), "ada_b": f(ada_b[0]).reshape(1, -1), "norm_g": f(norm_g[0]),
        "w_f1u": f(ffn1_up[0]), "w_f1d": f(ffn1_down[0]), "w_win": f(w_in[0]), "w_wba": f(w_branch_a[0]),
        "w_wbb": f(w_branch_b[0]), "w_wo": f(w_out[0]), "w_f2u": f(ffn2_up[0]), "w_f2d": f(ffn2_down[0]),
        "v_ln_g": f(v_ln_g[0]).reshape(1, -1), "v_ln_b": f(v_ln_b[0]).reshape(1, -1),
        "w_sp": f(w_spatial[0]), "b_sp": f(b_spatial[0]), "final_g": f(final_g).reshape(1, -1),
    }
    shared.update(_consts())
    x_prompt = np.asarray(x_prompt); x_sample = np.asarray(x_sample)
    in_maps = []
    for c in range(8):
        m = dict(shared)
        m["xp"] = f(x_prompt[2 * c:2 * c + 2])
        m["xs"] = f(x_sample[16 * c:16 * c + 16, 0])
        m["cv"] = f(np.concatenate([np.asarray(c_sample)[16 * c:16 * c + 16], np.asarray(c_prompt)[2 * c:2 * c + 2]], 0))
        m["ck0"] = f(np.asarray(cache_kv_g0)[0, 16 * c:16 * c + 16]).reshape(16, 128, 512)
        m["ck1"] = f(np.asarray(cache_kv_g1)[0, 16 * c:16 * c + 16]).reshape(16, 512, 512)
        m["ck2"] = f(np.asarray(cache_kv_g2)[0, 16 * c:16 * c + 16]).reshape(16, 2048, 512)
        in_maps.append(m)
    res = run_bass_kernel_spmd(nc, in_maps, core_ids=list(range(8)))
    R = res.results
    cat = lambda k: np.concatenate([np.asarray(r[k]) for r in R], 0)
    y_prompt = cat("yp")
    y_sample = cat("ys").reshape(128, 1, D)
    outs = [y_prompt, y_sample]
    for g, L in enumerate((128, 512, 2048)):
        outs.append(cat("kv%dp" % g).reshape(1, 16, L, 2, 4, 64))
    outs.append(cat("vrp").reshape(1, 16, 128, 512))
    for g, L in enumerate((128, 512, 2048)):
        outs.append(cat("kv%ds" % g).reshape(1, 128, L, 2, 4, 64))
    outs.append(cat("vrs").reshape(1, 128, 1, 512))
    return tuple(np.ascontiguousarray(o, dtype=np.float32) for o in outs)
```
